# Optimizing a Trainium2 kernel written in Bass

```python
import jax, jax.numpy as jnp
from jax import lax
import numpy as np

D_MODEL = 1024
BATCH = 4
SEQ = 8192
DEPTH = 1

LRU_WIDTH = D_MODEL * 5 // 4
LRU_HEADS = 10
LRU_HEAD_DIM = LRU_WIDTH // LRU_HEADS
LRU_C = 8.0
LRU_CONV = 4
POOL_WIDTH = D_MODEL
POOL_WINDOWS = (2, 4, 8, 16)
POOL_GROUPS = len(POOL_WINDOWS)
POOL_GROUP_DIM = POOL_WIDTH // POOL_GROUPS
N_BRANCHES = 2
IN_WIDTH = LRU_WIDTH + POOL_WIDTH + N_BRANCHES * D_MODEL
D_FF = D_MODEL * 11 // 4
FFN_CONV = 3
RMS_EPS = 1e-6

kernel_name = "hybrid_rglru_pool_gated_merge_block"


def rms_norm(x, g):
    x32 = x.astype(jnp.float32)
    y = x32 * lax.rsqrt(jnp.mean(x32 * x32, axis=-1, keepdims=True) + RMS_EPS)
    return (y * g.astype(jnp.float32)).astype(x.dtype)


def depthwise_conv(x, w, b, left):
    k_width = w.shape[0]
    s = x.shape[1]
    xp = jnp.pad(x, ((0, 0), (left, k_width - 1 - left), (0, 0)))
    y = b
    for k in range(k_width):
        y = y + w[k] * xp[:, k:k + s]
    return y


def block_diag(x, w, b):
    bsz, s, _ = x.shape
    h, i, j = w.shape
    y = jnp.einsum('bshi,hij->bshj', x.reshape(bsz, s, h, i), w)
    return y.reshape(bsz, s, h * j) + b


def _lin_combine(left, right):
    a_l, b_l = left
    a_r, b_r = right
    return a_l * a_r, a_r * b_l + b_r


def rglru_direction(xc, wa, ba, wx, bx, lam, reverse):
    x32 = xc.astype(jnp.float32)
    r = jax.nn.sigmoid(block_diag(xc, wa, ba).astype(jnp.float32))
    i = jax.nn.sigmoid(block_diag(xc, wx, bx).astype(jnp.float32))
    log_a = -LRU_C * r * jax.nn.softplus(-lam.astype(jnp.float32))
    a = jnp.exp(log_a)
    b_in = jnp.sqrt(-jnp.expm1(2.0 * log_a)) * (i * x32)
    _, h = lax.associative_scan(_lin_combine, (a, b_in), axis=1, reverse=reverse)
    return h


def multiscale_pool(u):
    bsz, s, width = u.shape
    u32 = u.astype(jnp.float32)
    csum = jnp.concatenate([jnp.zeros((bsz, 1, width), jnp.float32),
                            jnp.cumsum(u32, axis=1)], axis=1)
    t = jnp.arange(s)
    outs = []
    for g, w in enumerate(POOL_WINDOWS):
        sl = slice(g * POOL_GROUP_DIM, (g + 1) * POOL_GROUP_DIM)
        cg = csum[..., sl]
        lo = jnp.clip(t - w // 2, 0, s - 1)
        hi = jnp.clip(t + w // 2 - 1, 0, s - 1)
        win_sum = jnp.take(cg, hi + 1, axis=1) - jnp.take(cg, lo, axis=1)
        count = (hi - lo + 1).astype(jnp.float32)[None, :, None]
        outs.append(win_sum / count - u32[..., sl])
    return jnp.concatenate(outs, axis=-1).astype(u.dtype)


def setup_inputs(seed: int = 0) -> dict:
    key = jax.random.key(seed)
    ks = jax.random.split(key, 32)
    f32 = jnp.float32
    L = DEPTH

    def nrm(k, shape, scale):
        return jax.random.normal(k, shape, f32) * scale

    def gain(k, shape):
        return 1.0 + 0.05 * jax.random.normal(k, shape, f32)

    def lru_lambda(k):
        u = jax.random.uniform(k, (L, LRU_WIDTH), f32, 0.9, 0.999)
        s = u ** (1.0 / LRU_C)
        return jnp.log(s) - jnp.log1p(-s)

    return {
        "x": jax.random.normal(ks[0], (BATCH, SEQ, D_MODEL), f32),
        "g_mix": gain(ks[1], (L, D_MODEL)),
        "w_in": nrm(ks[2], (L, D_MODEL, IN_WIDTH), D_MODEL ** -0.5),
        "b_gate": nrm(ks[3], (L, N_BRANCHES * D_MODEL), 0.01),
        "conv_a_w": nrm(ks[4], (L, LRU_CONV, LRU_WIDTH), LRU_CONV ** -0.5),
        "conv_a_b": nrm(ks[5], (L, LRU_WIDTH), 0.01),
        "wa_f": nrm(ks[6], (L, LRU_HEADS, LRU_HEAD_DIM, LRU_HEAD_DIM), LRU_HEAD_DIM ** -0.5),
        "ba_f": nrm(ks[7], (L, LRU_WIDTH), 0.01),
        "wx_f": nrm(ks[8], (L, LRU_HEADS, LRU_HEAD_DIM, LRU_HEAD_DIM), LRU_HEAD_DIM ** -0.5),
        "bx_f": nrm(ks[9], (L, LRU_WIDTH), 0.01),
        "lam_f": lru_lambda(ks[10]),
        "wa_b": nrm(ks[11], (L, LRU_HEADS, LRU_HEAD_DIM, LRU_HEAD_DIM), LRU_HEAD_DIM ** -0.5),
        "ba_b": nrm(ks[12], (L, LRU_WIDTH), 0.01),
        "wx_b": nrm(ks[13], (L, LRU_HEADS, LRU_HEAD_DIM, LRU_HEAD_DIM), LRU_HEAD_DIM ** -0.5),
        "bx_b": nrm(ks[14], (L, LRU_WIDTH), 0.01),
        "lam_b": lru_lambda(ks[15]),
        "w_pool": nrm(ks[16], (L, POOL_GROUPS, POOL_GROUP_DIM, POOL_GROUP_DIM), POOL_GROUP_DIM ** -0.5),
        "b_pool": nrm(ks[17], (L, POOL_WIDTH), 0.01),
        "pool_scale": gain(ks[18], (L, POOL_WIDTH)),
        "p_a": nrm(ks[19], (L, LRU_WIDTH, D_MODEL), LRU_WIDTH ** -0.5),
        "p_b": nrm(ks[20], (L, POOL_WIDTH, D_MODEL), POOL_WIDTH ** -0.5),
        "w_out": nrm(ks[21], (L, D_MODEL, D_MODEL), D_MODEL ** -0.5),
        "g_ffn": gain(ks[22], (L, D_MODEL)),
        "w_up": nrm(ks[23], (L, D_MODEL, 2 * D_FF), D_MODEL ** -0.5),
        "conv_f_w": nrm(ks[24], (L, FFN_CONV, 2 * D_FF), FFN_CONV ** -0.5),
        "conv_f_b": nrm(ks[25], (L, 2 * D_FF), 0.01),
        "w_down": nrm(ks[26], (L, D_FF, D_MODEL), D_FF ** -0.5),
        "g_final": gain(ks[27], (D_MODEL,)),
    }


def reference(x, g_mix, w_in, b_gate, conv_a_w, conv_a_b, wa_f, ba_f, wx_f, bx_f, lam_f,
              wa_b, ba_b, wx_b, bx_b, lam_b, w_pool, b_pool, pool_scale, p_a, p_b, w_out,
              g_ffn, w_up, conv_f_w, conv_f_b, w_down, g_final):
    for l in range(DEPTH):
        h = rms_norm(x, g_mix[l])
        proj = h @ w_in[l]
        u_a = proj[..., :LRU_WIDTH]
        u_b = proj[..., LRU_WIDTH:LRU_WIDTH + POOL_WIDTH]
        gate_logits = proj[..., LRU_WIDTH + POOL_WIDTH:] + b_gate[l]

        xa = depthwise_conv(u_a, conv_a_w[l], conv_a_b[l], LRU_CONV // 2)
        h_fwd = rglru_direction(xa, wa_f[l], ba_f[l], wx_f[l], bx_f[l], lam_f[l], False)
        h_bwd = rglru_direction(xa, wa_b[l], ba_b[l], wx_b[l], bx_b[l], lam_b[l], True)
        y_a = (h_fwd + h_bwd).astype(x.dtype) @ p_a[l]

        pooled = multiscale_pool(u_b)
        y_b = (block_diag(pooled, w_pool[l], b_pool[l]) * pool_scale[l]) @ p_b[l]

        gates = jax.nn.sigmoid(gate_logits.astype(jnp.float32)).astype(x.dtype)
        merged = gates[..., :D_MODEL] * y_a + gates[..., D_MODEL:] * y_b
        x = x + merged @ w_out[l]

        h = rms_norm(x, g_ffn[l])
        up = depthwise_conv(h @ w_up[l], conv_f_w[l], conv_f_b[l], FFN_CONV // 2)
        act = jax.nn.gelu(up[..., :D_FF], approximate=True) * up[..., D_FF:]
        x = x + act @ w_down[l]
    return rms_norm(x, g_final)
```

```python
import numpy as np
from contextlib import ExitStack
import concourse.bass as bass
import concourse.mybir as mybir
from concourse.bass_utils import run_bass_kernel_spmd

F32 = mybir.dt.float32
BF16 = mybir.dt.bfloat16
AF = mybir.ActivationFunctionType
ALU = mybir.AluOpType

D = 1024
S = 8192
LW = 1280
NH = 10
PW = 1024
DFF = 2816
NFF = 22
EPS = 1e-6
WINS = (2, 4, 8, 16)

PADL = 32
TOT = S + 64
HALF = S // 2
V1 = 482
NW1 = 17
V2 = 456
NW2 = 9
NB = 488
NB2 = 460
PIECE = 4096
NPIECE = 33
RSLOTS = 5
NSET = 4


class B:
    def __init__(self, key, ap):
        self.key = key
        self.ap = ap

    def __getitem__(self, idx):
        return B(self.key, self.ap[idx])


class Prog:
    ENGS = ("pe", "act", "dve", "pool", "sp")

    def __init__(self):
        self.ops = {e: [] for e in self.ENGS}
        self.count = {e: 0 for e in self.ENGS}
        self.writer = {}
        self.readers = {}
        self.waited = {e: {} for e in self.ENGS}
        self.dma_count = {}
        self.sems = set(self.ENGS)
        self.stage = ""
        self.tags = {e: [] for e in self.ENGS}

    def add(self, eng, fn, reads=(), writes=(), signal=True, stream=None):
        reads = list(reads)
        writes = list(writes)
        if stream is not None:
            writes.append("stream:" + stream)
        deps = []
        for k in reads + writes:
            t = self.writer.get(k)
            if t is not None:
                deps.append(t)
        for k in writes:
            deps.extend(self.readers.get(k, ()))
        if stream is not None:
            s = "dma:" + eng + ":" + stream
            self.sems.add(s)
            c = self.dma_count.get(s, 0) + 1
            self.dma_count[s] = c
            tok = (s, 16 * c)
            inc = (s, 16)
        elif signal:
            self.count[eng] += 1
            tok = (eng, self.count[eng])
            inc = (eng, 1)
        else:
            tok = (eng, self.count[eng] + 1)
            inc = None
        need = {}
        for (s, v) in deps:
            if eng == "pe" and s == "pe":
                continue
            if self.waited[eng].get(s, 0) >= v:
                continue
            if need.get(s, 0) < v:
                need[s] = v
        for s, v in need.items():
            self.waited[eng][s] = v
        self.ops[eng].append((list(need.items()), fn, inc))
        self.tags[eng].append(self.stage)
        for k in reads:
            self.readers.setdefault(k, []).append(tok)
        for k in writes:
            self.writer[k] = tok
            self.readers[k] = []
        return tok


def build_nc(STOP=99):
    nc = bass.Bass("TRN2", target_bir_lowering=False)

    def din(name, shape):
        return nc.dram_tensor(name, list(shape), F32, kind="ExternalInput").ap()

    xT = din("xT", [D, TOT])
    wstream_f = din("wstream_f", [NPIECE, 128, PIECE])
    wua_f = din("wua_f", [128, 8 * LW])
    wg1_f = din("wg1_f", [128, 2 * NH * 128])
    wg2_f = din("wg2_f", [128, 2 * NH * 128])
    vec = din("vec", [128, 512])
    mtab = din("mtab", [128, 8 * 16])
    ident = din("ident", [128, 128])
    outT = nc.dram_tensor("outT", [D, HALF], F32, kind="ExternalOutput").ap()
    wstream = nc.dram_tensor("wstream", [NPIECE, 128, PIECE], BF16, kind="Internal").ap()
    h1scr = nc.dram_tensor("h1scr", [LW, TOT], F32, kind="Internal").ap()
    xascr = nc.dram_tensor("xascr", [LW, TOT], F32, kind="Internal").ap()

    VC = {}
    off = 0
    for name, n in (("g_mix", 8), ("g_ffn", 8), ("g_final", 8), ("bg", 16), ("cab", 10), ("cat", 50),
                    ("ba1", 10), ("bx1", 10), ("lam1", 10), ("ba2", 10), ("bx2", 10), ("lam2", 10),
                    ("bpool", 8), ("pscale", 8), ("ktap", 8 * 17), ("cft", 44 * 3), ("cfb", 44)):
        VC[name] = off
        off += n
    assert off <= 512

    P = Prog()
    with ExitStack() as es:
        def sbt(name, shape, dt):
            return es.enter_context(nc.sbuf_tensor(name, list(shape), dt))

        vec_t = sbt("vec_t", [128, 512], F32)
        der_t = sbt("der_t", [128, 128], F32)
        wg2_t = sbt("wg2_t", [128, 2 * NH * 128], BF16)
        cst_t = sbt("cst_t", [128, 8], F32)
        ones_t = sbt("ones_t", [128, 128], BF16)
        wg1_t = sbt("wg1_t", [128, 2 * NH * 128], BF16)
        mtab_t = sbt("mtab_t", [128, 8 * 16], F32)
        xw_t = sbt("xw_t", [128, 8 * NB], F32)
        hT_t = sbt("hT_t", [128, 8 * NB], BF16)
        sq_t = [sbt(f"sq{i}", [128, NB], BF16) for i in range(2)]
        rstd_t = sbt("rstd_t", [128, NB], F32)
        lset = [{n: sbt(f"l{n}{s}", [128, NB], F32) for n in ("u", "xa", "tr", "ti", "a2", "h1")}
                for s in range(NSET)]
        lx16 = [sbt(f"lx16_{s}", [128, NB], BF16) for s in range(NSET)]
        lu16 = [sbt(f"lu16_{s}", [128, NB], BF16) for s in range(NSET)]
        ident_t = sbt("ident_t", [128, 128], F32)
        dg5_t = sbt("dg5_t", [128, 50 * 128], BF16)
        dg3_t = sbt("dg3_t", [128, 12 * 128], BF16)
        hsum_t = sbt("hsum_t", [128, NH * NB2], BF16)
        carry_t = sbt("carry_t", [128, 64], F32)
        pooled_t = sbt("pooled_t", [128, 8 * NB2], BF16)
        vb_t = sbt("vb_t", [128, 8 * NB2], BF16)
        tg_t = sbt("tg_t", [128, 16 * NB2], BF16)
        act_t = sbt("act_t", [128, NFF * V2], BF16)
        wslot_t = [sbt(f"wslot{i}", [128, PIECE], BF16) for i in range(RSLOTS)]
        ps_t = [es.enter_context(nc.psum_tensor(f"ps{i}", [128, 512], F32)) for i in range(8)]

        def T(key, t):
            return B(key, t[:])

        vecb = T("vec", vec_t)
        der = T("der", der_t)
        cst = T("cst", cst_t)
        ones = T("ones", ones_t)
        wg1 = T("wg1", wg1_t)
        wg2 = T("wg2", wg2_t)
        mtabb = T("mtab", mtab_t)
        xw = [B(f"xw{k}", xw_t[:, k * NB:(k + 1) * NB]) for k in range(8)]
        hT = [B(f"hT{k}", hT_t[:, k * NB:(k + 1) * NB]) for k in range(8)]
        hTb = [B(f"hTb{k}", tg_t[:, k * 2 * NB2:k * 2 * NB2 + NB]) for k in range(8)]
        sq = [T(f"sq{i}", sq_t[i]) for i in range(2)]
        rstd = T("rstd", rstd_t)
        LS = [{n: T(f"l{n}{s}", lset[s][n]) for n in lset[s]} for s in range(NSET)]
        for s in range(NSET):
            LS[s]["x16"] = T(f"lx16_{s}", lx16[s])
            LS[s]["u16"] = T(f"lu16_{s}", lu16[s])
        actf = act_t[:].bitcast(F32)
        poolf = pooled_t[:].bitcast(F32)
        vbf = vb_t[:].bitcast(F32)
        NS1 = NSET + 2
        LS1 = list(LS)
        s4 = {n_: B(f"l{n_}4", actf[:, i_ * NB:(i_ + 1) * NB]) for i_, n_ in enumerate(("xa", "tr", "ti", "a2", "h1"))}
        s4["x16"] = B("lx16_4", act_t[:, 10 * NB:11 * NB])
        s4["u16"] = B("lu16_4", act_t[:, 11 * NB:12 * NB])
        s5 = {n_: B(f"l{n_}5", poolf[:, i_ * NB:(i_ + 1) * NB]) for i_, n_ in enumerate(("xa", "tr", "ti"))}
        s5.update({n_: B(f"l{n_}5", vbf[:, i_ * NB:(i_ + 1) * NB]) for i_, n_ in enumerate(("a2", "h1"))})
        s5["x16"] = B("lx16_5", act_t[:, 12 * NB:13 * NB])
        s5["u16"] = B("lu16_5", act_t[:, 13 * NB:14 * NB])
        LS1 += [s4, s5]
        assert 14 * NB <= NFF * V2 and 3 * NB <= 4 * NB2
        tgf = tg_t[:].bitcast(F32)
        hsf = hsum_t[:].bitcast(F32)
        xal = [B(f"xal{k}", tgf[:, k * NB:(k + 1) * NB]) for k in range(7)] + [B("xal7", hsf[:, 0:NB])]
        rstd2 = B("rstd2", vbf[:, 0:NB])
        assert 7 * NB <= 8 * NB2
        PF_REAL = [f"tg{k}" for k in range(16)] + [f"hsum{j_}" for j_ in range(NH)] + [f"vb{k}" for k in range(8)]
        PF_ALIAS = [f"xal{k}" for k in range(8)] + ["rstd2"]
        hsum = [B(f"hsum{j}", hsum_t[:, j * NB2:(j + 1) * NB2]) for j in range(NH)]
        carry = T("carry", carry_t)
        identb = T("ident", ident_t)
        dg5 = [B(f"dg5_{i}", dg5_t[:, i * 128:(i + 1) * 128]) for i in range(50)]
        dg3 = [B(f"dg3_{i}", dg3_t[:, i * 128:(i + 1) * 128]) for i in range(12)]
        dg_rr = [0, 0]
        pooled = [B(f"pooled{k}", pooled_t[:, k * NB2:(k + 1) * NB2]) for k in range(8)]
        vb = [B(f"vb{k}", vb_t[:, k * NB2:(k + 1) * NB2]) for k in range(8)]
        tg = [B(f"tg{k}", tg_t[:, k * NB2:(k + 1) * NB2]) for k in range(16)]
        actb = [B(f"act{j}", act_t[:, j * V2:(j + 1) * V2]) for j in range(NFF)]
        wslot = [T(f"wslot{i}", wslot_t[i]) for i in range(RSLOTS)]
        psb = [T(f"ps{i}", ps_t[i]) for i in range(8)]
        ps_rr = [0]

        def PS():
            b = psb[ps_rr[0] % 6]
            ps_rr[0] += 1
            return b

        def vcol(name, i=0):
            c = VC[name] + i
            return vecb[:, c:c + 1]

        DC = {"hc1": 0, "c1": 10, "hc2": 20, "c2": 30, "hba1": 40, "hbx1": 50, "hba2": 60, "hbx2": 70, "hbg": 80}

        def dcol(name, i):
            c = DC[name] + i
            return der[:, c:c + 1]

        C_EPS, C_Q, C_ONE, C_ZERO = 0, 1, 2, 3

        def ccol(i):
            return cst[:, i:i + 1]

        def keys(*xs):
            return [x.key for x in xs if isinstance(x, B)]

        def apv(x):
            return x.ap if isinstance(x, B) else x

        def mm(out, lhsT, rhs, start, stop, sig=False):
            P.add("pe", lambda e: e.matmul(out.ap, lhsT=lhsT.ap, rhs=rhs.ap, start=start, stop=stop),
                  reads=keys(lhsT, rhs), writes=keys(out), signal=(stop or sig))

        def act(out, in_, func, bias=None, scale=None):
            kw = {}
            if bias is not None:
                kw["bias"] = apv(bias)
            if scale is not None:
                kw["scale"] = apv(scale)
            P.add("act", lambda e: e.activation(out=out.ap, in_=in_.ap, func=func, **kw),
                  reads=keys(in_, bias, scale), writes=keys(out))

        def ts(out, in0, s1, s2, op0, op1=None, eng="dve"):
            if op1 is None:
                f = lambda e: e.tensor_scalar(out=out.ap, in0=in0.ap, scalar1=apv(s1), scalar2=None, op0=op0)
            else:
                f = lambda e: e.tensor_scalar(out=out.ap, in0=in0.ap, scalar1=apv(s1), scalar2=apv(s2),
                                              op0=op0, op1=op1)
            P.add(eng, f, reads=keys(in0, s1, s2), writes=keys(out))

        def stt(out, in0, sc, in1, op0, op1):
            P.add("dve", lambda e: e.scalar_tensor_tensor(out=out.ap, in0=in0.ap, scalar=apv(sc), in1=in1.ap,
                                                          op0=op0, op1=op1),
                  reads=keys(in0, sc, in1), writes=keys(out))

        def tt(out, in0, in1, op, eng="dve"):
            P.add(eng, lambda e: e.tensor_tensor(out=out.ap, in0=in0.ap, in1=in1.ap, op=op),
                  reads=keys(in0, in1), writes=keys(out))

        def cp(out, in_, eng="dve"):
            P.add(eng, lambda e: e.tensor_copy(out=out.ap, in_=in_.ap), reads=keys(in_), writes=keys(out))

        def recip(out, in_):
            P.add("dve", lambda e: e.reciprocal(out=out.ap, in_=in_.ap), reads=keys(in_), writes=keys(out))

        def mset(out, val, eng="dve"):
            P.add(eng, lambda e: e.memset(out.ap, val), writes=keys(out))

        def scan(out, d0, d1, init):
            P.add("dve", lambda e: e.tensor_tensor_scan(out=out.ap, data0=d0.ap, data1=d1.ap, initial=apv(init),
                                                        op0=ALU.mult, op1=ALU.add),
                  reads=keys(d0, d1, init), writes=keys(out))

        def dma(eng, out, in_, stream, extra_reads=(), extra_writes=()):
            P.add(eng, lambda e: e.dma_start(out=out.ap, in_=in_.ap),
                  reads=keys(in_) + list(extra_reads), writes=keys(out) + list(extra_writes), stream=stream)

        dma("sp", vecb, B("d:vec", vec[:, :]), "vec")
        dma("sp", mtabb, B("d:mtab", mtab[:, :]), "mtab")
        dma("sp", identb, B("d:ident", ident[:, :]), "ident")
        dma("pool", wg1, B("d:wg1", wg1_f[:, :]), "wg1")
        dma("pool", wg2, B("d:wg2", wg2_f[:, :]), "wg2")
        for i, (k0, k1) in enumerate(((0, 3), (3, 6), (6, 8))):
            dma("pool", wslot[i][:, 0:(k1 - k0) * LW], B("d:wua", wua_f[:, k0 * LW:k1 * LW]), f"wslot{i}")
        def prologue_cast(i):
            dma("pool", B(f"d:ws{i}", wstream[i]), B("d:wsf", wstream_f[i]), f"pro{i % 4}")
        for i_ in range(50):
            ts(dg5[i_], identb, vcol("cat", i_), None, ALU.mult)
        mset(cst[:, C_EPS:C_EPS + 1], EPS)
        mset(cst[:, C_Q:C_Q + 1], 0.25)
        mset(cst[:, C_ONE:C_ONE + 1], 1.0)
        mset(cst[:, C_ZERO:C_ZERO + 1], 0.0)
        mset(ones, 1.0)
        mset(carry, 0.0)
        for di, nm in ((1, "lam1"), (2, "lam2")):
            lam = vecb[:, VC[nm]:VC[nm] + 10]
            cdst = der[:, DC[f"c{di}"]:DC[f"c{di}"] + 10]
            hdst = der[:, DC[f"hc{di}"]:DC[f"hc{di}"] + 10]
            act(cdst, lam, AF.Exp, scale=-1.0)
            act(cdst, cdst, AF.Ln, bias=ccol(C_ONE))
            ts(hdst, cdst, -4.0, None, ALU.mult)
            ts(cdst, cdst, -8.0, None, ALU.mult)
        for nm_, cnt in (("ba1", 10), ("bx1", 10), ("ba2", 10), ("bx2", 10), ("bg", 16)):
            ts(der[:, DC["h" + nm_]:DC["h" + nm_] + cnt], vecb[:, VC[nm_]:VC[nm_] + cnt], 0.5, None, ALU.mult)

        ssb = psb[7]
        ssb2 = psb[6]
        sq_rr = [0]

        def ss_add(src_k, n, first, last, bank=None):
            P.stage = "norm"
            bank = ssb if bank is None else bank
            s_ = sq[sq_rr[0] % 2]
            sq_rr[0] += 1
            act(s_[:, 0:n], src_k, AF.Square)
            mm(bank[:, 0:n], ones, s_[:, 0:n], first, last, sig=True)

        def ss_finish(src, n, gname, dst, bank=None, rs=None):
            P.stage = "norm"
            bank = ssb if bank is None else bank
            rs = rstd if rs is None else rs
            act(rs[:, 0:n], bank[:, 0:n], AF.Sqrt, bias=ccol(C_EPS), scale=1.0 / D)
            recip(rs[:, 0:n], rs[:, 0:n])
            for k in range(8):
                stt(dst[k], src[k], vcol(gname, k), rs[:, 0:n], ALU.mult, ALU.mult)

        def rmsnorm(src, n, gname, dst, dst_off=0):
            for k in range(8):
                ss_add(src[k], n, k == 0, k == 7)
            ss_finish(src, n, gname, dst)

        def diag5(j, o):
            return dg5[j * 5 + o]

        def diag3(ch, o):
            dg = dg3[dg_rr[1] % 12]
            dg_rr[1] += 1
            ts(dg, identb, vcol("cft", ch * 3 + o), None, ALU.mult)
            return dg

        def lru_S0(ps, st, nu, on_act=False):
            P.stage = "S0"
            if on_act:
                act(st["u16"][:, 0:nu], ps[:, 0:nu], AF.Copy)
            else:
                cp(st["u16"][:, 0:nu], ps[:, 0:nu])

        def lru_S1(st, j, nout, u_off):
            P.stage = "S1conv5"
            cps = PS()
            for o in range(5):
                dg = diag5(j, o)
                mm(cps[:, 0:nout], dg, st["u16"][:, u_off - 2 + o: u_off - 2 + o + nout], o == 0, o == 4)
            act(st["xa"][:, 0:nout], cps[:, 0:nout], AF.Identity, bias=vcol("cab", j))
            cp(st["x16"][:, 0:nout], st["xa"][:, 0:nout])

        def lru_B1(st, j, nout, wa, wx, di, on_act=False):
            P.stage = "S2gates"
            pr = PS()
            pi = PS()
            mm(pr[:, 0:nout], wa, st["x16"][:, 0:nout], True, True)
            mm(pi[:, 0:nout], wx, st["x16"][:, 0:nout], True, True)
            act(st["tr"][:, 0:nout], pr[:, 0:nout], AF.Tanh, bias=dcol(f"hba{di}", j), scale=0.5)
            act(st["ti"][:, 0:nout], pi[:, 0:nout], AF.Tanh, bias=dcol(f"hbx{di}", j), scale=0.5)
            act(st["tr"][:, 0:nout], st["tr"][:, 0:nout], AF.Exp, bias=dcol(f"hc{di}", j), scale=dcol(f"hc{di}", j))
            if on_act:
                act(st["a2"][:, 0:nout], st["tr"][:, 0:nout], AF.Square)
            else:
                tt(st["a2"][:, 0:nout], st["tr"][:, 0:nout], st["tr"][:, 0:nout], ALU.mult)
            stt(st["ti"][:, 0:nout], st["ti"][:, 0:nout], 1.0, st["xa"][:, 0:nout], ALU.add, ALU.mult)

        def lru_B2(st, nout):
            P.stage = "S3scan"
            act(st["a2"][:, 0:nout], st["a2"][:, 0:nout], AF.Sqrt, bias=ccol(C_Q), scale=-0.25)
            tt(st["a2"][:, 0:nout], st["a2"][:, 0:nout], st["ti"][:, 0:nout], ALU.mult)

        def wua(k, j):
            i, kk = (0, k) if k < 3 else ((1, k - 3) if k < 6 else (2, k - 6))
            return wslot[i][:, kk * LW + j * 128: kk * LW + (j + 1) * 128]

        h1_store_wins = {}
        nw1 = NW1 if STOP >= 2 else (1 if STOP == 1 else 0)

        def s1_geom(w):
            v0 = V1 * w
            v1 = min(v0 + V1, S)
            return v0, v1, v1 - v0, v1 - v0 + 4, v0 - 2 + PADL

        def s1_load(w):
            v0, v1, nv, n, c0 = s1_geom(w)
            for k in range(8):
                dma("sp", xw[k][:, 0:n], B("d:x", xT[k * 128:(k + 1) * 128, c0:c0 + n]), f"xw{k}")

        def s1_norm(w):
            v0, v1, nv, n, c0 = s1_geom(w)
            hs = hT if w % 2 == 0 else hTb
            rmsnorm([xw[k][:, 0:n] for k in range(8)], n, "g_mix", [hs[k][:, 0:n] for k in range(8)])

        if nw1:
            s1_load(0)
            s1_norm(0)
        for w in range(nw1):
            v0, v1, nv, n, c0 = s1_geom(w)
            hTw = hT if w % 2 == 0 else hTb
            if w + 1 < nw1:
                s1_load(w + 1)
            for i_ in range(3 * w, min(3 * w + 3, NPIECE)):
                prologue_cast(i_)
            store = v1 > HALF - 32

            def S0(j):
                st = LS1[j % NS1]
                ps = PS()
                for k in range(8):
                    mm(ps[:, 0:n], wua(k, j), hTw[k][:, 0:n], k == 0, k == 7)
                lru_S0(ps, st, n)

            def S1(j):
                st = LS1[j % NS1]
                lru_S1(st, j, nv, 2)
                if store:
                    dma("pool", B(f"d:xa:{j}:{w}", xascr[j * 128:(j + 1) * 128, v0 + PADL:v1 + PADL]),
                        st["xa"][:, 0:nv], f"lxa{j % NS1}")

            def S2(j):
                st = LS1[j % NS1]
                lru_B1(st, j, nv, wg1[:, j * 128:(j + 1) * 128], wg1[:, (NH + j) * 128:(NH + j + 1) * 128], 1)

            def S3(j):
                st = LS1[j % NS1]
                lru_B2(st, nv)
                scan(st["h1"][:, 0:nv], st["tr"][:, 0:nv], st["a2"][:, 0:nv], carry[:, j:j + 1])
                cp(carry[:, j:j + 1], st["h1"][:, nv - 1:nv])
                if store:
                    dma("pool", B(f"d:h1:{j}:{w}", h1scr[j * 128:(j + 1) * 128, v0 + PADL:v1 + PADL]),
                        st["h1"][:, 0:nv], f"lh1{j % NS1}")

            steps = []
            npair = NH // 2
            for q in range(npair + 2):
                if q < npair:
                    steps += [(S0, 2 * q), (S0, 2 * q + 1)]
                if 0 <= q - 1 < npair:
                    steps += [(S1, 2 * q - 2), (S1, 2 * q - 1)]
                if 0 <= q - 2 < npair:
                    steps += [(S2, 2 * q - 4), (S2, 2 * q - 3), (S3, 2 * q - 4), (S3, 2 * q - 3)]
            if STOP == 1:
                import os
                steps = steps[:int(os.environ.get("SUB", "99"))]
            for si_, (fn_, j_) in enumerate(steps):
                if si_ == 26 and w + 1 < nw1:
                    s1_norm(w + 1)
                fn_(j_)
            if store:
                h1_store_wins[w] = (v0, v1)
        for i_ in range(3 * nw1, NPIECE):
            prologue_cast(i_)
        P.add("dve", lambda e: e.memset(cst_t[:, 7:8], 0.0),
              writes=[f"hTb{k}" for k in range(8)] + [f"tg{k}" for k in range(16)] + ["cst7"]
              + [b_.key for b_ in list(s4.values()) + list(s5.values())]
              + [f"act{j_}" for j_ in range(NFF)] + [f"pooled{k}" for k in range(8)] + [f"vb{k}" for k in range(8)])

        class WStream:
            def __init__(self, total):
                self.total = total
                self.nxt = 0
                self.loaded = 0
                for _ in range(RSLOTS - 1):
                    self._load()

            def _load(self):
                p = self.loaded
                if p >= self.total:
                    return
                pi = p % NPIECE
                dma("sp", wslot[p % RSLOTS], B(f"d:ws{pi}", wstream[pi]), f"wslot{p % RSLOTS}")
                self.loaded += 1

            def get(self):
                p = self.nxt
                while self.loaded < min(p + RSLOTS - 1, self.total) or self.loaded <= p:
                    self._load()
                self.nxt += 1
                return wslot[p % RSLOTS]

        WS = WStream(NW2 * NPIECE) if STOP >= 3 else None

        mset(carry[:, 16:16 + NH], 0.0)
        out_keys = []
        for w in range(NW2 if STOP >= 4 else (1 if STOP == 3 else 0)):
            v1 = S - V2 * w
            v0 = max(v1 - V2, HALF)
            nf = v1 - v0
            nm = nf + 2
            n = nf + 18
            c0 = v0 - 9 + PADL
            top = (w == 0)
            ns = nm - 1 if top else nm - 2
            nh = nm - 1 if top else nm
            for k in range(8):
                dma("sp", xw[k][:, 0:n], B("d:x", xT[k * 128:(k + 1) * 128, c0:c0 + n]), f"xw{k}")
            if w == 0:
                rmsnorm([xw[k][:, 0:n] for k in range(8)], n, "g_mix", [hT[k][:, 0:n] for k in range(8)])
            nw2 = NW2 if STOP >= 4 else (1 if STOP == 3 else 0)
            has_next = (w + 1 < nw2)
            if has_next:
                v1n = S - V2 * (w + 1)
                v0n = max(v1n - V2, HALF)
                nn = v1n - v0n + 18
                c0n = v0n - 9 + PADL

            wpiece = [None]

            def win_chunk(oc, n=n, wpiece=wpiece):
                P.stage = "win"
                if oc % 4 == 0:
                    wpiece[0] = WS.get()
                ps = PS()
                q = oc % 4
                for k in range(8):
                    mm(ps[:, 0:n], wpiece[0][:, k * 512 + q * 128: k * 512 + (q + 1) * 128], hT[k][:, 0:n],
                       k == 0, k == 7)
                return ps

            def SL(j):
                P.stage = "S0"
                st = LS[j % NSET]
                t0 = v0 - 1
                t1 = t0 + nh
                rk = [f"d:xa:{j}:{ww}" for ww, (a0, a1) in h1_store_wins.items() if a0 < t1 and a1 > t0]
                dma("pool", st["xa"][:, 0:nh], B("d:xascr", xascr[j * 128:(j + 1) * 128, t0 + PADL:t1 + PADL]),
                    f"lxa{j % NSET}", extra_reads=rk)
                if top:
                    mset(st["xa"][:, nh:nm], 0.0)
                cp(st["x16"][:, 0:nm], st["xa"][:, 0:nm])

            def S2(j):
                st = LS[j % NSET]
                lru_B1(st, j, nm, wg2[:, j * 128:(j + 1) * 128], wg2[:, (NH + j) * 128:(NH + j + 1) * 128], 2,
                       on_act=True)

            def S3(j):
                st = LS[j % NSET]
                lru_B2(st, nm)
                t0 = v0 - 1
                t1 = t0 + nh
                rk = [f"d:h1:{j}:{ww}" for ww, (a0, a1) in h1_store_wins.items() if a0 < t1 and a1 > t0]
                dma("pool", st["h1"][:, 0:nh], B("d:h1scr", h1scr[j * 128:(j + 1) * 128, t0 + PADL:t1 + PADL]),
                    f"lh1{j % NSET}", extra_reads=rk)
                h2 = st["u"]
                scan(h2[:, 0:ns][:, ::-1], st["tr"][:, 0:ns][:, ::-1], st["a2"][:, 0:ns][:, ::-1],
                     carry[:, 16 + j:17 + j])
                if not top:
                    cp(h2[:, ns:nm], carry[:, 32 + 2 * j:34 + 2 * j])
                cp(carry[:, 16 + j:17 + j], h2[:, 0:1])
                cp(carry[:, 32 + 2 * j:34 + 2 * j], h2[:, 0:2])
                tt(hsum[j][:, 0:nh], h2[:, 0:nh], st["h1"][:, 0:nh], ALU.add)
                if top:
                    mset(hsum[j][:, nh:nm], 0.0)

            for q in range(NH // 2 + 1):
                if q < NH // 2:
                    SL(2 * q)
                    SL(2 * q + 1)
                if q >= 1:
                    S2(2 * q - 2)
                    S2(2 * q - 1)
                    S3(2 * q - 2)
                    S3(2 * q - 1)

            P.stage = "pool"
            for c in range(8):
                ps = win_chunk(c)
                P.stage = "pool"
                u_ = LS[c % NSET]["u"]
                Sb = LS[c % NSET]["ti"]
                act(u_[:, 0:n], ps[:, 0:n], AF.Copy)
                wv = WINS[c // 2]
                bufs = [LS[c % NSET]["tr"], LS[c % NSET]["a2"]]
                cur, width, span, bi = u_, n, 1, 0
                while span < wv:
                    nxt = bufs[bi]
                    bi ^= 1
                    width -= span
                    tt(nxt[:, 0:width], cur[:, 0:width], cur[:, span:span + width], ALU.add)
                    cur = nxt
                    span *= 2
                b0 = 8 - wv // 2
                ts(Sb[:, 0:nm], cur[:, b0:b0 + nm], vcol("ktap", 2 * c), None, ALU.mult)
                stt(Sb[:, 0:nm], cur[:, b0 + 1:b0 + 1 + nm], vcol("ktap", 2 * c + 1), Sb[:, 0:nm], ALU.mult, ALU.add)
                if top:
                    tt(Sb[:, nm - 16:nm], Sb[:, nm - 16:nm], mtabb[:, c * 16:(c + 1) * 16], ALU.mult)
                tt(pooled[c][:, 0:nm], Sb[:, 0:nm], u_[:, 8:8 + nm], ALU.subtract)

            for k in range(16):
                ps = win_chunk(8 + k)
                P.stage = "tg"
                act(tg[k][:, 0:nm], ps[:, 8:8 + nm], AF.Tanh, bias=dcol("hbg", k), scale=0.5)

            P.stage = "ya"
            for i in range(3):
                wp = WS.get()
                for dd in range(3):
                    d = 3 * i + dd
                    if d >= 8:
                        continue
                    ps = PS()
                    for k in range(NH):
                        lo = k * 384 + dd * 128
                        mm(ps[:, 0:nm], wp[:, lo:lo + 128], hsum[k][:, 0:nm], k == 0, k == NH - 1)
                    stt(actb[d][:, 0:nm - 2], tg[d][:, 0:nm - 2], 1.0, ps[:, 0:nm - 2], ALU.add, ALU.mult)
                    stt(actb[8 + d][:, 0:2], tg[d][:, nm - 2:nm], 1.0, ps[:, nm - 2:nm], ALU.add, ALU.mult)
            P.stage = "wpool"
            wp = WS.get()
            for g in range(4):
                for oc in range(2):
                    ps = PS()
                    for kc in range(2):
                        lo = (g * 2 + kc) * 256 + oc * 128
                        mm(ps[:, 0:nm], wp[:, lo:lo + 128], pooled[2 * g + kc][:, 0:nm], kc == 0, kc == 1)
                    d = 2 * g + oc
                    ts(vb[d][:, 0:nm], ps[:, 0:nm], vcol("bpool", d), vcol("pscale", d), ALU.add, ALU.mult)

            P.stage = "yb"
            for i in range(2):
                wp = WS.get()
                for dd in range(4):
                    d = 4 * i + dd
                    ps = PS()
                    for k in range(8):
                        lo = k * 512 + dd * 128
                        mm(ps[:, 0:nm], wp[:, lo:lo + 128], vb[k][:, 0:nm], k == 0, k == 7)
                    m_ = LS[d % NSET]["tr"]
                    stt(m_[:, 0:nm], tg[8 + d][:, 0:nm], 1.0, ps[:, 0:nm], ALU.add, ALU.mult)
                    tt(hsum[d][:, 0:nm - 2], m_[:, 0:nm - 2], actb[d][:, 0:nm - 2], ALU.add)
                    tt(hsum[d][:, nm - 2:nm], m_[:, nm - 2:nm], actb[8 + d][:, 0:2], ALU.add)
            P.stage = "wout"
            for i in range(2):
                wp = WS.get()
                for dd in range(4):
                    d = 4 * i + dd
                    ps = PS()
                    for k in range(8):
                        lo = k * 512 + dd * 128
                        mm(ps[:, 0:nm], wp[:, lo:lo + 128], hsum[k][:, 0:nm], k == 0, k == 7)
                    stt(xw[d][:, 8:8 + nm], ps[:, 0:nm], 0.5, xw[d][:, 8:8 + nm], ALU.mult, ALU.add)
                    if d >= 2:
                        ss_add(xw[d - 2][:, 8:8 + nm], nm, d == 2, False)
                    P.stage = "wout"
            ss_add(xw[6][:, 8:8 + nm], nm, False, False)
            ss_add(xw[7][:, 8:8 + nm], nm, False, True)

            ss_finish([xw[k][:, 8:8 + nm] for k in range(8)], nm, "g_ffn", [hT[k][:, 0:nm] for k in range(8)])
            if top:
                for k in range(8):
                    mset(hT[k][:, nm - 1:nm], 0.0)
            if has_next:
                for k in range(8):
                    dma("sp", xal[k][:, 0:nn], B("d:x", xT[k * 128:(k + 1) * 128, c0n:c0n + nn]), f"xal{k}",
                        extra_writes=(PF_REAL + ["rstd2"]) if k == 0 else [])
            ups = {}
            cres = {}
            wpf = [None]

            def FU(h):
                if h % 4 == 0:
                    wpf[0] = WS.get()
                P.stage = "up"
                jj, half = h // 2, h % 2
                q = half * 2 + (jj % 2)
                ps = PS()
                for k in range(8):
                    mm(ps[:, 0:nm], wpf[0][:, k * 512 + q * 128:k * 512 + (q + 1) * 128], hT[k][:, 0:nm],
                       k == 0, k == 7)
                u16 = LS[h % NSET]["u16"]
                act(u16[:, 0:nm], ps[:, 0:nm], AF.Copy)
                ups[h] = u16

            dgs = {}

            def FD(h):
                P.stage = "conv3"
                jj, half = h // 2, h % 2
                ch = half * NFF + jj
                dgs[h] = [diag3(ch, o) for o in range(3)]

            def FC(h):
                P.stage = "conv3"
                c_ = PS()
                for o in range(3):
                    mm(c_[:, 0:nf], dgs[h][o], ups[h][:, o:o + nf], o == 0, o == 2)
                cres[h] = c_

            def FE(jj):
                P.stage = "ffn_e"
                gl = LS[jj % NSET]["xa"]
                act(gl[:, 0:nf], cres[2 * jj][:, 0:nf], AF.Gelu_apprx_tanh, bias=vcol("cfb", jj))
                stt(actb[jj][:, 0:nf], cres[2 * jj + 1][:, 0:nf], vcol("cfb", NFF + jj), gl[:, 0:nf],
                    ALU.add, ALU.mult)

            FD(0)
            FD(1)
            FU(0)
            FU(1)
            for h in range(2 * NFF):
                if h + 2 < 2 * NFF:
                    FD(h + 2)
                    FU(h + 2)
                FC(h)
                if h % 2 == 1:
                    FE(h // 2)
            if has_next:
                for k in range(8):
                    ss_add(xal[k][:, 0:nn], nn, k == 0, k == 7, bank=ssb2)
                ss_finish([xal[k][:, 0:nn] for k in range(8)], nn, "g_mix", [hT[k][:, 0:nn] for k in range(8)],
                          bank=ssb2, rs=rstd2)
                P.add("dve", lambda e: e.memset(cst_t[:, 6:7], 0.0), writes=PF_REAL + PF_ALIAS + ["cst6"])
            P.stage = "down"
            for d in range(8):
                wp = WS.get()
                ps = PS()
                for kc in range(NFF):
                    mm(ps[:, 0:nf], wp[:, kc * 128:(kc + 1) * 128], actb[kc][:, 0:nf], kc == 0, kc == NFF - 1)
                tt(xw[d][:, 9:9 + nf], ps[:, 0:nf], xw[d][:, 9:9 + nf], ALU.add)
                if d >= 1:
                    ss_add(xw[d - 1][:, 9:9 + nf], nf, d == 1, False)
                P.stage = "down"
            ss_add(xw[7][:, 9:9 + nf], nf, False, True)
            ss_finish([xw[k][:, 9:9 + nf] for k in range(8)], nf, "g_final", [xw[k][:, 9:9 + nf] for k in range(8)])
            for k in range(8):
                ok = f"d:out:{w}:{k}"
                out_keys.append(ok)
                dma("pool", B(ok, outT[k * 128:(k + 1) * 128, v0 - HALF:v1 - HALF]), xw[k][:, 9:9 + nf], f"xw{k}")

        P.add("pool", lambda e: None, reads=out_keys + [k for k in P.writer if k.startswith("stream:")],
              signal=False)

        sems = {s: es.enter_context(nc.semaphore(s.replace(":", "_"))) for s in sorted(P.sems)}
        block = es.enter_context(nc.Block())

        def runner(engops):
            def f(e):
                for waits, fn, inc in engops:
                    for s, v in waits:
                        e.wait_ge(sems[s], v)
                    ins = fn(e)
                    if inc is not None and ins is not None:
                        ins.then_inc(sems[inc[0]], inc[1])
            return f

        block.sync(runner(P.ops["sp"]))
        block.scalar(runner(P.ops["act"]))
        block.vector(runner(P.ops["dve"]))
        block.gpsimd(runner(P.ops["pool"]))
        block.tensor(runner(P.ops["pe"]))
        print("ops:", {k: len(v) for k, v in P.ops.items()}, "sems:", len(sems), flush=True)
        import os
        if os.environ.get("DUMP_TAGS"):
            import json
            json.dump(P.tags, open(os.environ["DUMP_TAGS"], "w"))
    return nc


def _cols(v, nch):
    return np.ascontiguousarray(np.asarray(v, np.float32).reshape(nch, 128).T)


def _wstream(w_in, w_pool, p_a, p_b, w_out, w_up, w_down):
    ws = np.zeros((NPIECE, 128, PIECE), np.float32)
    pi = 0
    wi = w_in[:, LW:].reshape(8, 128, -1)
    for i in range(6):
        c0, c1 = 512 * i, min(512 * (i + 1), wi.shape[2])
        blk = np.zeros((128, 8, 512), np.float32)
        blk[:, :, :c1 - c0] = wi[:, :, c0:c1].transpose(1, 0, 2)
        ws[pi] = blk.reshape(128, -1)
        pi += 1
    pa = p_a.reshape(NH, 128, 8, 128)
    for i in range(3):
        blk = np.zeros((128, NH, 3, 128), np.float32)
        dn = min(3, 8 - 3 * i)
        blk[:, :, :dn] = pa[:, :, 3 * i:3 * i + dn].transpose(1, 0, 2, 3)
        ws[pi, :, :NH * 384] = blk.reshape(128, -1)
        pi += 1
    blk = w_pool.reshape(4, 2, 128, 256).transpose(2, 0, 1, 3)
    ws[pi, :, :2048] = blk.reshape(128, -1)
    pi += 1
    for m in (p_b, w_out):
        mm_ = m.reshape(8, 128, 2, 512)
        for i in range(2):
            ws[pi] = mm_[:, :, i].transpose(1, 0, 2).reshape(128, -1)
            pi += 1
    wu = w_up.reshape(8, 128, 2, NFF, 128)
    for i in range(11):
        blk = wu[:, :, :, 2 * i:2 * i + 2]
        ws[pi] = blk.transpose(1, 0, 2, 3, 4).reshape(128, -1)
        pi += 1
    wd = w_down.reshape(NFF, 128, 8, 128)
    for d in range(8):
        ws[pi, :, :NFF * 128] = wd[:, :, d].transpose(1, 0, 2).reshape(128, -1)
        pi += 1
    assert pi == NPIECE
    return ws


def kernel(x, g_mix, w_in, b_gate, conv_a_w, conv_a_b, wa_f, ba_f, wx_f, bx_f, lam_f,
           wa_b, ba_b, wx_b, bx_b, lam_b, w_pool, b_pool, pool_scale, p_a, p_b, w_out,
           g_ffn, w_up, conv_f_w, conv_f_b, w_down, g_final):
    f = lambda a: np.asarray(a, np.float32)
    x = f(x)
    w_in0 = f(w_in)[0]
    ws = _wstream(w_in0, f(w_pool)[0], f(p_a)[0], f(p_b)[0], f(w_out)[0], f(w_up)[0], f(w_down)[0])
    wua = np.ascontiguousarray(w_in0[:, :LW].reshape(8, 128, LW).transpose(1, 0, 2).reshape(128, -1))

    def gatepack(wa, wx):
        a = f(wa)[0].transpose(1, 0, 2)
        b = f(wx)[0].transpose(1, 0, 2)
        return np.ascontiguousarray(np.concatenate([a, b], axis=1).reshape(128, -1))

    in_maps = []
    for c in range(8):
        b, p = c // 2, c % 2
        seq = x[b] if p == 1 else x[b][::-1]
        xT = np.zeros((D, TOT), np.float32)
        xT[:, PADL:PADL + S] = seq.T
        if p == 1:
            d1 = (wa_f, ba_f, wx_f, bx_f, lam_f)
            d2 = (wa_b, ba_b, wx_b, bx_b, lam_b)
        else:
            d1 = (wa_b, ba_b, wx_b, bx_b, lam_b)
            d2 = (wa_f, ba_f, wx_f, bx_f, lam_f)
        caw = f(conv_a_w)[0]
        taps5 = np.zeros((5, LW), np.float32)
        if p == 1:
            taps5[0:4] = caw
        else:
            taps5[1:5] = caw[::-1]
        cfw = f(conv_f_w)[0]
        taps3 = cfw if p == 1 else cfw[::-1]
        vec = np.zeros((128, 512), np.float32)
        off = [0]

        def put(arr):
            a = np.asarray(arr, np.float32)
            vec[:, off[0]:off[0] + a.shape[1]] = a
            off[0] += a.shape[1]

        put(_cols(f(g_mix)[0], 8))
        put(_cols(f(g_ffn)[0], 8))
        put(_cols(f(g_final), 8))
        put(_cols(f(b_gate)[0], 16))
        put(_cols(f(conv_a_b)[0], 10))
        put(taps5.T.reshape(NH, 128, 5).transpose(1, 0, 2).reshape(128, 50))
        put(_cols(f(d1[1])[0], 10))
        put(_cols(f(d1[3])[0], 10))
        put(_cols(f(d1[4])[0], 10))
        put(_cols(f(d2[1])[0], 10))
        put(_cols(f(d2[3])[0], 10))
        put(_cols(f(d2[4])[0], 10))
        put(_cols(f(b_pool)[0], 8))
        put(_cols(f(pool_scale)[0], 8))
        kt = np.zeros((128, 8 * 17), np.float32)
        for cidx in range(8):
            wv = WINS[cidx // 2]
            kt[:, 2 * cidx + (0 if p == 1 else 1)] = 1.0 / wv
        put(kt.reshape(128, -1))
        put(taps3.T.reshape(2 * NFF, 128, 3).transpose(1, 0, 2).reshape(128, -1))
        put(_cols(f(conv_f_b)[0], 2 * NFF))
        mt = np.ones((128, 8, 16), np.float32)
        for cidx in range(8):
            wv = WINS[cidx // 2]
            for m in range(16):
                tau = S - 15 + m
                if tau >= S:
                    continue
                t = tau if p == 1 else S - 1 - tau
                lo = min(max(t - wv // 2, 0), S - 1)
                hi = min(max(t + wv // 2 - 1, 0), S - 1)
                mt[:, cidx, m] = wv / float(hi - lo + 1)
        in_maps.append({
            "xT": xT, "wstream_f": ws, "wua_f": wua,
            "wg1_f": gatepack(d1[0], d1[2]), "wg2_f": gatepack(d2[0], d2[2]),
            "vec": vec, "mtab": np.ascontiguousarray(mt.reshape(128, -1)),
            "ident": np.eye(128, dtype=np.float32),
        })
    nc = build_nc()
    res = run_bass_kernel_spmd(nc, in_maps, core_ids=list(range(8)))
    out = np.zeros((4, S, D), np.float32)
    for c in range(8):
        b, p = c // 2, c % 2
        o = np.asarray(res.results[c]["outT"]).T
        if p == 1:
            out[b, HALF:] = o
        else:
            out[b, :HALF] = o[::-1]
    return out
```

```python
import numpy as np
from contextlib import ExitStack
import concourse.bass as bass
import concourse.mybir as mybir
from concourse.bass_utils import run_bass_kernel_spmd

F32 = mybir.dt.float32
BF16 = mybir.dt.bfloat16
AF = mybir.ActivationFunctionType
ALU = mybir.AluOpType

D = 1024
S = 8192
LW = 1280
NH = 10
PW = 1024
DFF = 2816
NFF = 22
EPS = 1e-6
WINS = (2, 4, 8, 16)

PADL = 32
TOT = S + 64
HALF = S // 2
V1 = 482
NW1 = 17
V2 = 456
NW2 = 9
NB = 488
NB2 = 460
PIECE = 4096
NPIECE = 33
RSLOTS = 5
NSET = 4
WSEQ = [("b", 0), ("g", 0), ("g", 1), ("b", 1), ("g", 2), ("b", 2), ("g", 3), ("b", 3),
        ("g", 4), ("b", 4), ("g", 5), ("b", 5), ("g", 6), ("b", 6), ("g", 7), ("b", 7)] \
    + [("g", k) for k in range(8, 16)]


class B:
    def __init__(self, key, ap):
        self.key = key
        self.ap = ap

    def __getitem__(self, idx):
        return B(self.key, self.ap[idx])


class Prog:
    ENGS = ("pe", "act", "dve", "pool", "sp")

    def __init__(self):
        self.ops = {e: [] for e in self.ENGS}
        self.count = {e: 0 for e in self.ENGS}
        self.writer = {}
        self.readers = {}
        self.waited = {e: {} for e in self.ENGS}
        self.dma_count = {}
        self.sems = set(self.ENGS)
        self.stage = ""
        self.tags = {e: [] for e in self.ENGS}

    def add(self, eng, fn, reads=(), writes=(), signal=True, stream=None):
        reads = list(reads)
        writes = list(writes)
        if stream is not None:
            writes.append("stream:" + stream)
        deps = []
        for k in reads + writes:
            t = self.writer.get(k)
            if t is not None:
                deps.append(t)
        for k in writes:
            deps.extend(self.readers.get(k, ()))
        if stream is not None:
            s = "dma:" + eng + ":" + stream
            self.sems.add(s)
            c = self.dma_count.get(s, 0) + 1
            self.dma_count[s] = c
            tok = (s, 16 * c)
            inc = (s, 16)
        elif signal:
            self.count[eng] += 1
            tok = (eng, self.count[eng])
            inc = (eng, 1)
        else:
            tok = (eng, self.count[eng] + 1)
            inc = None
        need = {}
        for (s, v) in deps:
            if eng == "pe" and s == "pe":
                continue
            if self.waited[eng].get(s, 0) >= v:
                continue
            if need.get(s, 0) < v:
                need[s] = v
        for s, v in need.items():
            self.waited[eng][s] = v
        self.ops[eng].append((list(need.items()), fn, inc))
        self.tags[eng].append(self.stage)
        for k in reads:
            self.readers.setdefault(k, []).append(tok)
        for k in writes:
            self.writer[k] = tok
            self.readers[k] = []
        return tok


def build_nc(STOP=99):
    nc = bass.Bass("TRN2", target_bir_lowering=False)

    def din(name, shape):
        return nc.dram_tensor(name, list(shape), F32, kind="ExternalInput").ap()

    xT = din("xT", [D, TOT])
    wstream_f = din("wstream_f", [NPIECE, 128, PIECE])
    wua_f = din("wua_f", [128, 8 * LW])
    wg1_f = din("wg1_f", [128, 2 * NH * 128])
    wg2_f = din("wg2_f", [128, 2 * NH * 128])
    vec = din("vec", [128, 512])
    mtab = din("mtab", [128, 8 * 16])
    ident = din("ident", [128, 128])
    outT = nc.dram_tensor("outT", [D, HALF], F32, kind="ExternalOutput").ap()
    wstream = nc.dram_tensor("wstream", [NPIECE, 128, PIECE], BF16, kind="Internal").ap()
    h1scr = nc.dram_tensor("h1scr", [LW, TOT], F32, kind="Internal").ap()
    xascr = nc.dram_tensor("xascr", [LW, TOT], F32, kind="Internal").ap()

    VC = {}
    off = 0
    for name, n in (("g_mix", 8), ("g_ffn", 8), ("g_final", 8), ("bg", 16), ("cab", 10), ("cat", 50),
                    ("ba1", 10), ("bx1", 10), ("lam1", 10), ("ba2", 10), ("bx2", 10), ("lam2", 10),
                    ("bpool", 8), ("pscale", 8), ("ktap", 8 * 17), ("cft", 44 * 3), ("cfb", 44)):
        VC[name] = off
        off += n
    assert off <= 512

    P = Prog()
    with ExitStack() as es:
        def sbt(name, shape, dt):
            return es.enter_context(nc.sbuf_tensor(name, list(shape), dt))

        vec_t = sbt("vec_t", [128, 512], F32)
        der_t = sbt("der_t", [128, 128], F32)
        wg2_t = sbt("wg2_t", [128, 2 * NH * 128], BF16)
        cst_t = sbt("cst_t", [128, 8], F32)
        ones_t = sbt("ones_t", [128, 128], BF16)
        wg1_t = sbt("wg1_t", [128, 2 * NH * 128], BF16)
        mtab_t = sbt("mtab_t", [128, 8 * 16], F32)
        xw_t = sbt("xw_t", [128, 8 * NB], F32)
        hT_t = sbt("hT_t", [128, 8 * NB], BF16)
        sq_t = [sbt(f"sq{i}", [128, NB], BF16) for i in range(2)]
        rstd_t = sbt("rstd_t", [128, NB], F32)
        lset = [{n: sbt(f"l{n}{s}", [128, NB], F32) for n in ("u", "xa", "tr", "ti", "a2", "h1")}
                for s in range(NSET)]
        lx16 = [sbt(f"lx16_{s}", [128, NB], BF16) for s in range(NSET)]
        lu16 = [sbt(f"lu16_{s}", [128, NB], BF16) for s in range(NSET)]
        ident_t = sbt("ident_t", [128, 128], F32)
        dg5_t = sbt("dg5_t", [128, 50 * 128], BF16)
        dg3_t = sbt("dg3_t", [128, 12 * 128], BF16)
        hsum_t = sbt("hsum_t", [128, NH * NB2], BF16)
        carry_t = sbt("carry_t", [128, 64], F32)
        pooled_t = sbt("pooled_t", [128, 8 * NB2], BF16)
        vb_t = sbt("vb_t", [128, 8 * NB2], BF16)
        tg_t = sbt("tg_t", [128, 16 * NB2], BF16)
        act_t = sbt("act_t", [128, NFF * V2], BF16)
        wslot_t = [sbt(f"wslot{i}", [128, PIECE], BF16) for i in range(RSLOTS)]
        ps_t = [es.enter_context(nc.psum_tensor(f"ps{i}", [128, 512], F32)) for i in range(8)]

        def T(key, t):
            return B(key, t[:])

        vecb = T("vec", vec_t)
        der = T("der", der_t)
        cst = T("cst", cst_t)
        ones = T("ones", ones_t)
        wg1 = T("wg1", wg1_t)
        wg2 = T("wg2", wg2_t)
        mtabb = T("mtab", mtab_t)
        xw = [B(f"xw{k}", xw_t[:, k * NB:(k + 1) * NB]) for k in range(8)]
        hT = [B(f"hT{k}", hT_t[:, k * NB:(k + 1) * NB]) for k in range(8)]
        hTb = [B(f"hTb{k}", tg_t[:, k * 2 * NB2:k * 2 * NB2 + NB]) for k in range(8)]
        sq = [T(f"sq{i}", sq_t[i]) for i in range(2)]
        rstd = T("rstd", rstd_t)
        LS = [{n: T(f"l{n}{s}", lset[s][n]) for n in lset[s]} for s in range(NSET)]
        for s in range(NSET):
            LS[s]["x16"] = T(f"lx16_{s}", lx16[s])
            LS[s]["u16"] = T(f"lu16_{s}", lu16[s])
        actf = act_t[:].bitcast(F32)
        poolf = pooled_t[:].bitcast(F32)
        vbf = vb_t[:].bitcast(F32)
        NS1 = NSET + 2
        LS1 = list(LS)
        s4 = {n_: B(f"l{n_}4", actf[:, i_ * NB:(i_ + 1) * NB]) for i_, n_ in enumerate(("xa", "tr", "ti", "a2", "h1"))}
        s4["x16"] = B("lx16_4", act_t[:, 10 * NB:11 * NB])
        s4["u16"] = B("lu16_4", act_t[:, 11 * NB:12 * NB])
        s5 = {n_: B(f"l{n_}5", poolf[:, i_ * NB:(i_ + 1) * NB]) for i_, n_ in enumerate(("xa", "tr", "ti"))}
        s5.update({n_: B(f"l{n_}5", vbf[:, i_ * NB:(i_ + 1) * NB]) for i_, n_ in enumerate(("a2", "h1"))})
        s5["x16"] = B("lx16_5", act_t[:, 12 * NB:13 * NB])
        s5["u16"] = B("lu16_5", act_t[:, 13 * NB:14 * NB])
        LS1 += [s4, s5]
        assert 14 * NB <= NFF * V2 and 3 * NB <= 4 * NB2
        tgf = tg_t[:].bitcast(F32)
        hsf = hsum_t[:].bitcast(F32)
        xal = [B(f"xal{k}", tgf[:, k * NB:(k + 1) * NB]) for k in range(7)] + [B("xal7", hsf[:, 0:NB])]
        rstd2 = B("rstd2", vbf[:, 0:NB])
        assert 7 * NB <= 8 * NB2
        PF_REAL = [f"tg{k}" for k in range(16)] + [f"hsum{j_}" for j_ in range(NH)] + [f"vb{k}" for k in range(8)]
        PF_ALIAS = [f"xal{k}" for k in range(8)] + ["rstd2"]
        hsum = [B(f"hsum{j}", hsum_t[:, j * NB2:(j + 1) * NB2]) for j in range(NH)]
        carry = T("carry", carry_t)
        identb = T("ident", ident_t)
        dg5 = [B(f"dg5_{i}", dg5_t[:, i * 128:(i + 1) * 128]) for i in range(50)]
        dg3 = [B(f"dg3_{i}", dg3_t[:, i * 128:(i + 1) * 128]) for i in range(12)]
        dg_rr = [0, 0]
        pooled = [B(f"pooled{k}", pooled_t[:, k * NB2:(k + 1) * NB2]) for k in range(8)]
        vb = [B(f"vb{k}", vb_t[:, k * NB2:(k + 1) * NB2]) for k in range(8)]
        tg = [B(f"tg{k}", tg_t[:, k * NB2:(k + 1) * NB2]) for k in range(16)]
        actb = [B(f"act{j}", act_t[:, j * V2:(j + 1) * V2]) for j in range(NFF)]
        wslot = [T(f"wslot{i}", wslot_t[i]) for i in range(RSLOTS)]
        psb = [T(f"ps{i}", ps_t[i]) for i in range(8)]
        ps_rr = [0]

        def PS():
            b = psb[ps_rr[0] % 6]
            ps_rr[0] += 1
            return b

        def vcol(name, i=0):
            c = VC[name] + i
            return vecb[:, c:c + 1]

        DC = {"hc1": 0, "c1": 10, "hc2": 20, "c2": 30, "hba1": 40, "hbx1": 50, "hba2": 60, "hbx2": 70, "hbg": 80}

        def dcol(name, i):
            c = DC[name] + i
            return der[:, c:c + 1]

        C_EPS, C_Q, C_ONE, C_ZERO = 0, 1, 2, 3

        def ccol(i):
            return cst[:, i:i + 1]

        def keys(*xs):
            return [x.key for x in xs if isinstance(x, B)]

        def apv(x):
            return x.ap if isinstance(x, B) else x

        def mm(out, lhsT, rhs, start, stop, sig=False):
            P.add("pe", lambda e: e.matmul(out.ap, lhsT=lhsT.ap, rhs=rhs.ap, start=start, stop=stop),
                  reads=keys(lhsT, rhs), writes=keys(out), signal=(stop or sig))

        def act(out, in_, func, bias=None, scale=None):
            kw = {}
            if bias is not None:
                kw["bias"] = apv(bias)
            if scale is not None:
                kw["scale"] = apv(scale)
            P.add("act", lambda e: e.activation(out=out.ap, in_=in_.ap, func=func, **kw),
                  reads=keys(in_, bias, scale), writes=keys(out))

        def ts(out, in0, s1, s2, op0, op1=None, eng="dve"):
            if op1 is None:
                f = lambda e: e.tensor_scalar(out=out.ap, in0=in0.ap, scalar1=apv(s1), scalar2=None, op0=op0)
            else:
                f = lambda e: e.tensor_scalar(out=out.ap, in0=in0.ap, scalar1=apv(s1), scalar2=apv(s2),
                                              op0=op0, op1=op1)
            P.add(eng, f, reads=keys(in0, s1, s2), writes=keys(out))

        def stt(out, in0, sc, in1, op0, op1):
            P.add("dve", lambda e: e.scalar_tensor_tensor(out=out.ap, in0=in0.ap, scalar=apv(sc), in1=in1.ap,
                                                          op0=op0, op1=op1),
                  reads=keys(in0, sc, in1), writes=keys(out))

        def tt(out, in0, in1, op, eng="dve"):
            P.add(eng, lambda e: e.tensor_tensor(out=out.ap, in0=in0.ap, in1=in1.ap, op=op),
                  reads=keys(in0, in1), writes=keys(out))

        def cp(out, in_, eng="dve"):
            P.add(eng, lambda e: e.tensor_copy(out=out.ap, in_=in_.ap), reads=keys(in_), writes=keys(out))

        def recip(out, in_):
            P.add("dve", lambda e: e.reciprocal(out=out.ap, in_=in_.ap), reads=keys(in_), writes=keys(out))

        def mset(out, val, eng="dve"):
            P.add(eng, lambda e: e.memset(out.ap, val), writes=keys(out))

        def scan(out, d0, d1, init):
            P.add("dve", lambda e: e.tensor_tensor_scan(out=out.ap, data0=d0.ap, data1=d1.ap, initial=apv(init),
                                                        op0=ALU.mult, op1=ALU.add),
                  reads=keys(d0, d1, init), writes=keys(out))

        def dma(eng, out, in_, stream, extra_reads=(), extra_writes=()):
            P.add(eng, lambda e: e.dma_start(out=out.ap, in_=in_.ap),
                  reads=keys(in_) + list(extra_reads), writes=keys(out) + list(extra_writes), stream=stream)

        dma("sp", vecb, B("d:vec", vec[:, :]), "vec")
        dma("sp", mtabb, B("d:mtab", mtab[:, :]), "mtab")
        dma("sp", identb, B("d:ident", ident[:, :]), "ident")
        dma("pool", wg1, B("d:wg1", wg1_f[:, :]), "wg1")
        dma("pool", wg2, B("d:wg2", wg2_f[:, :]), "wg2")
        for i, (k0, k1) in enumerate(((0, 3), (3, 6), (6, 8))):
            dma("pool", wslot[i][:, 0:(k1 - k0) * LW], B("d:wua", wua_f[:, k0 * LW:k1 * LW]), f"wslot{i}")
        def prologue_cast(i):
            dma("pool", B(f"d:ws{i}", wstream[i]), B("d:wsf", wstream_f[i]), f"pro{i % 4}")
        for i_ in range(50):
            ts(dg5[i_], identb, vcol("cat", i_), None, ALU.mult)
        mset(cst[:, C_EPS:C_EPS + 1], EPS)
        mset(cst[:, C_Q:C_Q + 1], 0.25)
        mset(cst[:, C_ONE:C_ONE + 1], 1.0)
        mset(cst[:, C_ZERO:C_ZERO + 1], 0.0)
        mset(ones, 1.0)
        mset(carry, 0.0)
        for di, nm in ((1, "lam1"), (2, "lam2")):
            lam = vecb[:, VC[nm]:VC[nm] + 10]
            cdst = der[:, DC[f"c{di}"]:DC[f"c{di}"] + 10]
            hdst = der[:, DC[f"hc{di}"]:DC[f"hc{di}"] + 10]
            act(cdst, lam, AF.Exp, scale=-1.0)
            act(cdst, cdst, AF.Ln, bias=ccol(C_ONE))
            ts(hdst, cdst, -4.0, None, ALU.mult)
            ts(cdst, cdst, -8.0, None, ALU.mult)
        for nm_, cnt in (("ba1", 10), ("bx1", 10), ("ba2", 10), ("bx2", 10), ("bg", 16)):
            ts(der[:, DC["h" + nm_]:DC["h" + nm_] + cnt], vecb[:, VC[nm_]:VC[nm_] + cnt], 0.5, None, ALU.mult)

        ssb = psb[7]
        ssb2 = psb[6]
        sq_rr = [0]

        def ss_add(src_k, n, first, last, bank=None):
            P.stage = "norm"
            bank = ssb if bank is None else bank
            s_ = sq[sq_rr[0] % 2]
            sq_rr[0] += 1
            act(s_[:, 0:n], src_k, AF.Square)
            mm(bank[:, 0:n], ones, s_[:, 0:n], first, last, sig=True)

        def ss_finish(src, n, gname, dst, bank=None, rs=None):
            P.stage = "norm"
            bank = ssb if bank is None else bank
            rs = rstd if rs is None else rs
            act(rs[:, 0:n], bank[:, 0:n], AF.Sqrt, bias=ccol(C_EPS), scale=1.0 / D)
            recip(rs[:, 0:n], rs[:, 0:n])
            for k in range(8):
                stt(dst[k], src[k], vcol(gname, k), rs[:, 0:n], ALU.mult, ALU.mult)

        def rmsnorm(src, n, gname, dst, dst_off=0):
            for k in range(8):
                ss_add(src[k], n, k == 0, k == 7)
            ss_finish(src, n, gname, dst)

        def diag5(j, o):
            return dg5[j * 5 + o]

        def diag3(ch, o):
            dg = dg3[dg_rr[1] % 12]
            dg_rr[1] += 1
            ts(dg, identb, vcol("cft", ch * 3 + o), None, ALU.mult)
            return dg

        def lru_S0(ps, st, nu, on_act=False):
            P.stage = "S0"
            if on_act:
                act(st["u16"][:, 0:nu], ps[:, 0:nu], AF.Copy)
            else:
                cp(st["u16"][:, 0:nu], ps[:, 0:nu])

        def lru_S1(st, j, nout, u_off):
            P.stage = "S1conv5"
            cps = PS()
            for o in range(5):
                dg = diag5(j, o)
                mm(cps[:, 0:nout], dg, st["u16"][:, u_off - 2 + o: u_off - 2 + o + nout], o == 0, o == 4)
            act(st["xa"][:, 0:nout], cps[:, 0:nout], AF.Identity, bias=vcol("cab", j))
            cp(st["x16"][:, 0:nout], st["xa"][:, 0:nout])

        def lru_B1(st, j, nout, wa, wx, di, on_act=False):
            P.stage = "S2gates"
            pr = PS()
            pi = PS()
            mm(pr[:, 0:nout], wa, st["x16"][:, 0:nout], True, True)
            mm(pi[:, 0:nout], wx, st["x16"][:, 0:nout], True, True)
            act(st["tr"][:, 0:nout], pr[:, 0:nout], AF.Tanh, bias=dcol(f"hba{di}", j), scale=0.5)
            act(st["ti"][:, 0:nout], pi[:, 0:nout], AF.Tanh, bias=dcol(f"hbx{di}", j), scale=0.5)
            act(st["tr"][:, 0:nout], st["tr"][:, 0:nout], AF.Exp, bias=dcol(f"hc{di}", j), scale=dcol(f"hc{di}", j))
            if on_act:
                act(st["a2"][:, 0:nout], st["tr"][:, 0:nout], AF.Square)
            else:
                tt(st["a2"][:, 0:nout], st["tr"][:, 0:nout], st["tr"][:, 0:nout], ALU.mult)
            stt(st["ti"][:, 0:nout], st["ti"][:, 0:nout], 1.0, st["xa"][:, 0:nout], ALU.add, ALU.mult)

        def lru_B2(st, nout):
            P.stage = "S3scan"
            act(st["a2"][:, 0:nout], st["a2"][:, 0:nout], AF.Sqrt, bias=ccol(C_Q), scale=-0.25)
            tt(st["a2"][:, 0:nout], st["a2"][:, 0:nout], st["ti"][:, 0:nout], ALU.mult)

        def wua(k, j):
            i, kk = (0, k) if k < 3 else ((1, k - 3) if k < 6 else (2, k - 6))
            return wslot[i][:, kk * LW + j * 128: kk * LW + (j + 1) * 128]

        h1_store_wins = {}
        nw1 = NW1 if STOP >= 2 else (1 if STOP == 1 else 0)

        def s1_geom(w):
            v0 = V1 * w
            v1 = min(v0 + V1, S)
            return v0, v1, v1 - v0, v1 - v0 + 4, v0 - 2 + PADL

        def s1_load(w):
            v0, v1, nv, n, c0 = s1_geom(w)
            for k in range(8):
                dma("sp", xw[k][:, 0:n], B("d:x", xT[k * 128:(k + 1) * 128, c0:c0 + n]), f"xw{k}")

        def s1_norm(w):
            v0, v1, nv, n, c0 = s1_geom(w)
            hs = hT if w % 2 == 0 else hTb
            rmsnorm([xw[k][:, 0:n] for k in range(8)], n, "g_mix", [hs[k][:, 0:n] for k in range(8)])

        if nw1:
            s1_load(0)
            s1_norm(0)
        for w in range(nw1):
            v0, v1, nv, n, c0 = s1_geom(w)
            hTw = hT if w % 2 == 0 else hTb
            if w + 1 < nw1:
                s1_load(w + 1)
            for i_ in range(3 * w, min(3 * w + 3, NPIECE)):
                prologue_cast(i_)
            store = v1 > HALF - 32

            def S0(j):
                st = LS1[j % NS1]
                ps = PS()
                for k in range(8):
                    mm(ps[:, 0:n], wua(k, j), hTw[k][:, 0:n], k == 0, k == 7)
                lru_S0(ps, st, n)

            def S1(j):
                st = LS1[j % NS1]
                lru_S1(st, j, nv, 2)
                if store:
                    dma("pool", B(f"d:xa:{j}:{w}", xascr[j * 128:(j + 1) * 128, v0 + PADL:v1 + PADL]),
                        st["xa"][:, 0:nv], f"lxa{j % NS1}")

            def S2(j):
                st = LS1[j % NS1]
                lru_B1(st, j, nv, wg1[:, j * 128:(j + 1) * 128], wg1[:, (NH + j) * 128:(NH + j + 1) * 128], 1)

            def S3(j):
                st = LS1[j % NS1]
                lru_B2(st, nv)
                scan(st["h1"][:, 0:nv], st["tr"][:, 0:nv], st["a2"][:, 0:nv], carry[:, j:j + 1])
                cp(carry[:, j:j + 1], st["h1"][:, nv - 1:nv])
                if store:
                    dma("pool", B(f"d:h1:{j}:{w}", h1scr[j * 128:(j + 1) * 128, v0 + PADL:v1 + PADL]),
                        st["h1"][:, 0:nv], f"lh1{j % NS1}")

            steps = []
            npair = NH // 2
            for q in range(npair + 2):
                if q < npair:
                    steps += [(S0, 2 * q), (S0, 2 * q + 1)]
                if 0 <= q - 1 < npair:
                    steps += [(S1, 2 * q - 2), (S1, 2 * q - 1)]
                if 0 <= q - 2 < npair:
                    steps += [(S2, 2 * q - 4), (S2, 2 * q - 3), (S3, 2 * q - 4), (S3, 2 * q - 3)]
            if STOP == 1:
                import os
                steps = steps[:int(os.environ.get("SUB", "99"))]
            for si_, (fn_, j_) in enumerate(steps):
                if si_ == 26 and w + 1 < nw1:
                    s1_norm(w + 1)
                fn_(j_)
            if store:
                h1_store_wins[w] = (v0, v1)
        for i_ in range(3 * nw1, NPIECE):
            prologue_cast(i_)
        P.add("dve", lambda e: e.memset(cst_t[:, 7:8], 0.0),
              writes=[f"hTb{k}" for k in range(8)] + [f"tg{k}" for k in range(16)] + ["cst7"]
              + [b_.key for b_ in list(s4.values()) + list(s5.values())]
              + [f"act{j_}" for j_ in range(NFF)] + [f"pooled{k}" for k in range(8)] + [f"vb{k}" for k in range(8)])

        class WStream:
            def __init__(self, total):
                self.total = total
                self.nxt = 0
                self.loaded = 0
                for _ in range(RSLOTS - 1):
                    self._load()

            def _load(self):
                p = self.loaded
                if p >= self.total:
                    return
                pi = p % NPIECE
                dma("sp", wslot[p % RSLOTS], B(f"d:ws{pi}", wstream[pi]), f"wslot{p % RSLOTS}")
                self.loaded += 1

            def get(self):
                p = self.nxt
                while self.loaded < min(p + RSLOTS - 1, self.total) or self.loaded <= p:
                    self._load()
                self.nxt += 1
                return wslot[p % RSLOTS]

        WS = WStream(NW2 * NPIECE) if STOP >= 3 else None

        mset(carry[:, 16:16 + NH], 0.0)
        out_keys = []
        for w in range(NW2 if STOP >= 4 else (1 if STOP == 3 else 0)):
            v1 = S - V2 * w
            v0 = max(v1 - V2, HALF)
            nf = v1 - v0
            nm = nf + 2
            n = nf + 18
            c0 = v0 - 9 + PADL
            top = (w == 0)
            ns = nm - 1 if top else nm - 2
            nh = nm - 1 if top else nm
            for k in range(8):
                dma("sp", xw[k][:, 0:n], B("d:x", xT[k * 128:(k + 1) * 128, c0:c0 + n]), f"xw{k}")
            if w == 0:
                rmsnorm([xw[k][:, 0:n] for k in range(8)], n, "g_mix", [hT[k][:, 0:n] for k in range(8)])
            nw2 = NW2 if STOP >= 4 else (1 if STOP == 3 else 0)
            has_next = (w + 1 < nw2)
            if has_next:
                v1n = S - V2 * (w + 1)
                v0n = max(v1n - V2, HALF)
                nn = v1n - v0n + 18
                c0n = v0n - 9 + PADL

            wpiece = [None]

            def win_chunk(oc, n=n, wpiece=wpiece):
                P.stage = "win"
                if oc % 4 == 0:
                    wpiece[0] = WS.get()
                ps = PS()
                q = oc % 4
                for k in range(8):
                    mm(ps[:, 0:n], wpiece[0][:, k * 512 + q * 128: k * 512 + (q + 1) * 128], hT[k][:, 0:n],
                       k == 0, k == 7)
                return ps

            def SL(j):
                P.stage = "S0"
                st = LS[j % NSET]
                t0 = v0 - 1
                t1 = t0 + nh
                rk = [f"d:xa:{j}:{ww}" for ww, (a0, a1) in h1_store_wins.items() if a0 < t1 and a1 > t0]
                dma("pool", st["xa"][:, 0:nh], B("d:xascr", xascr[j * 128:(j + 1) * 128, t0 + PADL:t1 + PADL]),
                    f"lxa{j % NSET}", extra_reads=rk)
                if top:
                    mset(st["xa"][:, nh:nm], 0.0)
                cp(st["x16"][:, 0:nm], st["xa"][:, 0:nm])

            def S2(j):
                st = LS[j % NSET]
                lru_B1(st, j, nm, wg2[:, j * 128:(j + 1) * 128], wg2[:, (NH + j) * 128:(NH + j + 1) * 128], 2,
                       on_act=True)

            def S3(j):
                st = LS[j % NSET]
                lru_B2(st, nm)
                t0 = v0 - 1
                t1 = t0 + nh
                rk = [f"d:h1:{j}:{ww}" for ww, (a0, a1) in h1_store_wins.items() if a0 < t1 and a1 > t0]
                dma("pool", st["h1"][:, 0:nh], B("d:h1scr", h1scr[j * 128:(j + 1) * 128, t0 + PADL:t1 + PADL]),
                    f"lh1{j % NSET}", extra_reads=rk)
                h2 = st["u"]
                scan(h2[:, 0:ns][:, ::-1], st["tr"][:, 0:ns][:, ::-1], st["a2"][:, 0:ns][:, ::-1],
                     carry[:, 16 + j:17 + j])
                if not top:
                    cp(h2[:, ns:nm], carry[:, 32 + 2 * j:34 + 2 * j])
                cp(carry[:, 16 + j:17 + j], h2[:, 0:1])
                cp(carry[:, 32 + 2 * j:34 + 2 * j], h2[:, 0:2])
                tt(hsum[j][:, 0:nh], h2[:, 0:nh], st["h1"][:, 0:nh], ALU.add)
                if top:
                    mset(hsum[j][:, nh:nm], 0.0)

            def pool_chunk(pos, c):
                ps = win_chunk(pos)
                P.stage = "pool"
                u_ = LS[c % NSET]["u"]
                Sb = LS[c % NSET]["ti"]
                act(u_[:, 0:n], ps[:, 0:n], AF.Copy)
                wv = WINS[c // 2]
                bufs = [LS[c % NSET]["tr"], LS[c % NSET]["a2"]]
                cur, width, span, bi = u_, n, 1, 0
                while span < wv:
                    nxt = bufs[bi]
                    bi ^= 1
                    width -= span
                    tt(nxt[:, 0:width], cur[:, 0:width], cur[:, span:span + width], ALU.add)
                    cur = nxt
                    span *= 2
                b0 = 8 - wv // 2
                ts(Sb[:, 0:nm], cur[:, b0:b0 + nm], vcol("ktap", 2 * c), None, ALU.mult)
                stt(Sb[:, 0:nm], cur[:, b0 + 1:b0 + 1 + nm], vcol("ktap", 2 * c + 1), Sb[:, 0:nm], ALU.mult, ALU.add)
                if top:
                    tt(Sb[:, nm - 16:nm], Sb[:, nm - 16:nm], mtabb[:, c * 16:(c + 1) * 16], ALU.mult)
                tt(pooled[c][:, 0:nm], Sb[:, 0:nm], u_[:, 8:8 + nm], ALU.subtract)

            def gate_chunk(pos, k):
                ps = win_chunk(pos)
                P.stage = "tg"
                act(tg[k][:, 0:nm], ps[:, 8:8 + nm], AF.Tanh, bias=dcol("hbg", k), scale=0.5)

            for q in range(NH // 2 + 1):
                if q < NH // 2:
                    SL(2 * q)
                    SL(2 * q + 1)
                if q >= 1:
                    S2(2 * q - 2)
                    S2(2 * q - 1)
                    S3(2 * q - 2)
                    S3(2 * q - 1)
                for pos in range(4 * q, 4 * q + 4):
                    kind, idx = WSEQ[pos]
                    (pool_chunk if kind == "b" else gate_chunk)(pos, idx)

            P.stage = "ya"
            for i in range(3):
                wp = WS.get()
                for dd in range(3):
                    d = 3 * i + dd
                    if d >= 8:
                        continue
                    ps = PS()
                    for k in range(NH):
                        lo = k * 384 + dd * 128
                        mm(ps[:, 0:nm], wp[:, lo:lo + 128], hsum[k][:, 0:nm], k == 0, k == NH - 1)
                    stt(actb[d][:, 0:nm - 2], tg[d][:, 0:nm - 2], 1.0, ps[:, 0:nm - 2], ALU.add, ALU.mult)
                    stt(actb[8 + d][:, 0:2], tg[d][:, nm - 2:nm], 1.0, ps[:, nm - 2:nm], ALU.add, ALU.mult)
            P.stage = "wpool"
            wp = WS.get()
            for g in range(4):
                for oc in range(2):
                    ps = PS()
                    for kc in range(2):
                        lo = (g * 2 + kc) * 256 + oc * 128
                        mm(ps[:, 0:nm], wp[:, lo:lo + 128], pooled[2 * g + kc][:, 0:nm], kc == 0, kc == 1)
                    d = 2 * g + oc
                    ts(vb[d][:, 0:nm], ps[:, 0:nm], vcol("bpool", d), vcol("pscale", d), ALU.add, ALU.mult)

            P.stage = "yb"
            for i in range(2):
                wp = WS.get()
                for dd in range(4):
                    d = 4 * i + dd
                    ps = PS()
                    for k in range(8):
                        lo = k * 512 + dd * 128
                        mm(ps[:, 0:nm], wp[:, lo:lo + 128], vb[k][:, 0:nm], k == 0, k == 7)
                    m_ = LS[d % NSET]["tr"]
                    stt(m_[:, 0:nm], tg[8 + d][:, 0:nm], 1.0, ps[:, 0:nm], ALU.add, ALU.mult)
                    tt(hsum[d][:, 0:nm - 2], m_[:, 0:nm - 2], actb[d][:, 0:nm - 2], ALU.add)
                    tt(hsum[d][:, nm - 2:nm], m_[:, nm - 2:nm], actb[8 + d][:, 0:2], ALU.add)
            P.stage = "wout"
            for i in range(2):
                wp = WS.get()
                for dd in range(4):
                    d = 4 * i + dd
                    ps = PS()
                    for k in range(8):
                        lo = k * 512 + dd * 128
                        mm(ps[:, 0:nm], wp[:, lo:lo + 128], hsum[k][:, 0:nm], k == 0, k == 7)
                    stt(xw[d][:, 8:8 + nm], ps[:, 0:nm], 0.5, xw[d][:, 8:8 + nm], ALU.mult, ALU.add)
                    if d >= 2:
                        ss_add(xw[d - 2][:, 8:8 + nm], nm, d == 2, False)
                    P.stage = "wout"
            ss_add(xw[6][:, 8:8 + nm], nm, False, False)
            ss_add(xw[7][:, 8:8 + nm], nm, False, True)

            ss_finish([xw[k][:, 8:8 + nm] for k in range(8)], nm, "g_ffn", [hT[k][:, 0:nm] for k in range(8)])
            if top:
                for k in range(8):
                    mset(hT[k][:, nm - 1:nm], 0.0)
            if has_next:
                for k in range(8):
                    dma("sp", xal[k][:, 0:nn], B("d:x", xT[k * 128:(k + 1) * 128, c0n:c0n + nn]), f"xal{k}",
                        extra_writes=(PF_REAL + ["rstd2"]) if k == 0 else [])
            ups = {}
            cres = {}
            wpf = [None]

            def FU(h):
                if h % 4 == 0:
                    wpf[0] = WS.get()
                P.stage = "up"
                jj, half = h // 2, h % 2
                q = half * 2 + (jj % 2)
                ps = PS()
                for k in range(8):
                    mm(ps[:, 0:nm], wpf[0][:, k * 512 + q * 128:k * 512 + (q + 1) * 128], hT[k][:, 0:nm],
                       k == 0, k == 7)
                u16 = LS[h % NSET]["u16"]
                act(u16[:, 0:nm], ps[:, 0:nm], AF.Copy)
                ups[h] = u16

            dgs = {}

            def FD(h):
                P.stage = "conv3"
                jj, half = h // 2, h % 2
                ch = half * NFF + jj
                dgs[h] = [diag3(ch, o) for o in range(3)]

            def FC(h):
                P.stage = "conv3"
                c_ = PS()
                for o in range(3):
                    mm(c_[:, 0:nf], dgs[h][o], ups[h][:, o:o + nf], o == 0, o == 2)
                cres[h] = c_

            def FE(jj):
                P.stage = "ffn_e"
                gl = LS[jj % NSET]["xa"]
                act(gl[:, 0:nf], cres[2 * jj][:, 0:nf], AF.Gelu_apprx_tanh, bias=vcol("cfb", jj))
                stt(actb[jj][:, 0:nf], cres[2 * jj + 1][:, 0:nf], vcol("cfb", NFF + jj), gl[:, 0:nf],
                    ALU.add, ALU.mult)

            FD(0)
            FD(1)
            FU(0)
            FU(1)
            for h in range(2 * NFF):
                if h + 2 < 2 * NFF:
                    FD(h + 2)
                    FU(h + 2)
                FC(h)
                if h % 2 == 1:
                    FE(h // 2)
            if has_next:
                for k in range(8):
                    ss_add(xal[k][:, 0:nn], nn, k == 0, k == 7, bank=ssb2)
                ss_finish([xal[k][:, 0:nn] for k in range(8)], nn, "g_mix", [hT[k][:, 0:nn] for k in range(8)],
                          bank=ssb2, rs=rstd2)
                P.add("dve", lambda e: e.memset(cst_t[:, 6:7], 0.0), writes=PF_REAL + PF_ALIAS + ["cst6"])
            P.stage = "down"
            for d in range(8):
                wp = WS.get()
                ps = PS()
                for kc in range(NFF):
                    mm(ps[:, 0:nf], wp[:, kc * 128:(kc + 1) * 128], actb[kc][:, 0:nf], kc == 0, kc == NFF - 1)
                tt(xw[d][:, 9:9 + nf], ps[:, 0:nf], xw[d][:, 9:9 + nf], ALU.add)
                if d >= 1:
                    ss_add(xw[d - 1][:, 9:9 + nf], nf, d == 1, False)
                P.stage = "down"
            ss_add(xw[7][:, 9:9 + nf], nf, False, True)
            ss_finish([xw[k][:, 9:9 + nf] for k in range(8)], nf, "g_final", [xw[k][:, 9:9 + nf] for k in range(8)])
            for k in range(8):
                ok = f"d:out:{w}:{k}"
                out_keys.append(ok)
                dma("pool", B(ok, outT[k * 128:(k + 1) * 128, v0 - HALF:v1 - HALF]), xw[k][:, 9:9 + nf], f"xw{k}")

        P.add("pool", lambda e: None, reads=out_keys + [k for k in P.writer if k.startswith("stream:")],
              signal=False)

        sems = {s: es.enter_context(nc.semaphore(s.replace(":", "_"))) for s in sorted(P.sems)}
        block = es.enter_context(nc.Block())

        def runner(engops):
            def f(e):
                for waits, fn, inc in engops:
                    for s, v in waits:
                        e.wait_ge(sems[s], v)
                    ins = fn(e)
                    if inc is not None and ins is not None:
                        ins.then_inc(sems[inc[0]], inc[1])
            return f

        block.sync(runner(P.ops["sp"]))
        block.scalar(runner(P.ops["act"]))
        block.vector(runner(P.ops["dve"]))
        block.gpsimd(runner(P.ops["pool"]))
        block.tensor(runner(P.ops["pe"]))
        print("ops:", {k: len(v) for k, v in P.ops.items()}, "sems:", len(sems), flush=True)
        import os
        if os.environ.get("DUMP_TAGS"):
            import json
            json.dump(P.tags, open(os.environ["DUMP_TAGS"], "w"))
    return nc


def _cols(v, nch):
    return np.ascontiguousarray(np.asarray(v, np.float32).reshape(nch, 128).T)


def _wstream(w_in, w_pool, p_a, p_b, w_out, w_up, w_down):
    ws = np.zeros((NPIECE, 128, PIECE), np.float32)
    pi = 0
    wi = w_in[:, LW:].reshape(8, 128, -1)
    for i in range(6):
        blk = np.zeros((128, 8, 4, 128), np.float32)
        for q_, (kind, idx) in enumerate(WSEQ[4 * i:4 * i + 4]):
            c0 = idx * 128 if kind == "b" else PW + idx * 128
            blk[:, :, q_, :] = wi[:, :, c0:c0 + 128].transpose(1, 0, 2)
        ws[pi] = blk.reshape(128, -1)
        pi += 1
    pa = p_a.reshape(NH, 128, 8, 128)
    for i in range(3):
        blk = np.zeros((128, NH, 3, 128), np.float32)
        dn = min(3, 8 - 3 * i)
        blk[:, :, :dn] = pa[:, :, 3 * i:3 * i + dn].transpose(1, 0, 2, 3)
        ws[pi, :, :NH * 384] = blk.reshape(128, -1)
        pi += 1
    blk = w_pool.reshape(4, 2, 128, 256).transpose(2, 0, 1, 3)
    ws[pi, :, :2048] = blk.reshape(128, -1)
    pi += 1
    for m in (p_b, w_out):
        mm_ = m.reshape(8, 128, 2, 512)
        for i in range(2):
            ws[pi] = mm_[:, :, i].transpose(1, 0, 2).reshape(128, -1)
            pi += 1
    wu = w_up.reshape(8, 128, 2, NFF, 128)
    for i in range(11):
        blk = wu[:, :, :, 2 * i:2 * i + 2]
        ws[pi] = blk.transpose(1, 0, 2, 3, 4).reshape(128, -1)
        pi += 1
    wd = w_down.reshape(NFF, 128, 8, 128)
    for d in range(8):
        ws[pi, :, :NFF * 128] = wd[:, :, d].transpose(1, 0, 2).reshape(128, -1)
        pi += 1
    assert pi == NPIECE
    return ws


def kernel(x, g_mix, w_in, b_gate, conv_a_w, conv_a_b, wa_f, ba_f, wx_f, bx_f, lam_f,
           wa_b, ba_b, wx_b, bx_b, lam_b, w_pool, b_pool, pool_scale, p_a, p_b, w_out,
           g_ffn, w_up, conv_f_w, conv_f_b, w_down, g_final):
    f = lambda a: np.asarray(a, np.float32)
    x = f(x)
    w_in0 = f(w_in)[0]
    ws = _wstream(w_in0, f(w_pool)[0], f(p_a)[0], f(p_b)[0], f(w_out)[0], f(w_up)[0], f(w_down)[0])
    wua = np.ascontiguousarray(w_in0[:, :LW].reshape(8, 128, LW).transpose(1, 0, 2).reshape(128, -1))

    def gatepack(wa, wx):
        a = f(wa)[0].transpose(1, 0, 2)
        b = f(wx)[0].transpose(1, 0, 2)
        return np.ascontiguousarray(np.concatenate([a, b], axis=1).reshape(128, -1))

    in_maps = []
    for c in range(8):
        b, p = c // 2, c % 2
        seq = x[b] if p == 1 else x[b][::-1]
        xT = np.zeros((D, TOT), np.float32)
        xT[:, PADL:PADL + S] = seq.T
        if p == 1:
            d1 = (wa_f, ba_f, wx_f, bx_f, lam_f)
            d2 = (wa_b, ba_b, wx_b, bx_b, lam_b)
        else:
            d1 = (wa_b, ba_b, wx_b, bx_b, lam_b)
            d2 = (wa_f, ba_f, wx_f, bx_f, lam_f)
        caw = f(conv_a_w)[0]
        taps5 = np.zeros((5, LW), np.float32)
        if p == 1:
            taps5[0:4] = caw
        else:
            taps5[1:5] = caw[::-1]
        cfw = f(conv_f_w)[0]
        taps3 = cfw if p == 1 else cfw[::-1]
        vec = np.zeros((128, 512), np.float32)
        off = [0]

        def put(arr):
            a = np.asarray(arr, np.float32)
            vec[:, off[0]:off[0] + a.shape[1]] = a
            off[0] += a.shape[1]

        put(_cols(f(g_mix)[0], 8))
        put(_cols(f(g_ffn)[0], 8))
        put(_cols(f(g_final), 8))
        put(_cols(f(b_gate)[0], 16))
        put(_cols(f(conv_a_b)[0], 10))
        put(taps5.T.reshape(NH, 128, 5).transpose(1, 0, 2).reshape(128, 50))
        put(_cols(f(d1[1])[0], 10))
        put(_cols(f(d1[3])[0], 10))
        put(_cols(f(d1[4])[0], 10))
        put(_cols(f(d2[1])[0], 10))
        put(_cols(f(d2[3])[0], 10))
        put(_cols(f(d2[4])[0], 10))
        put(_cols(f(b_pool)[0], 8))
        put(_cols(f(pool_scale)[0], 8))
        kt = np.zeros((128, 8 * 17), np.float32)
        for cidx in range(8):
            wv = WINS[cidx // 2]
            kt[:, 2 * cidx + (0 if p == 1 else 1)] = 1.0 / wv
        put(kt.reshape(128, -1))
        put(taps3.T.reshape(2 * NFF, 128, 3).transpose(1, 0, 2).reshape(128, -1))
        put(_cols(f(conv_f_b)[0], 2 * NFF))
        mt = np.ones((128, 8, 16), np.float32)
        for cidx in range(8):
            wv = WINS[cidx // 2]
            for m in range(16):
                tau = S - 15 + m
                if tau >= S:
                    continue
                t = tau if p == 1 else S - 1 - tau
                lo = min(max(t - wv // 2, 0), S - 1)
                hi = min(max(t + wv // 2 - 1, 0), S - 1)
                mt[:, cidx, m] = wv / float(hi - lo + 1)
        in_maps.append({
            "xT": xT, "wstream_f": ws, "wua_f": wua,
            "wg1_f": gatepack(d1[0], d1[2]), "wg2_f": gatepack(d2[0], d2[2]),
            "vec": vec, "mtab": np.ascontiguousarray(mt.reshape(128, -1)),
            "ident": np.eye(128, dtype=np.float32),
        })
    nc = build_nc()
    res = run_bass_kernel_spmd(nc, in_maps, core_ids=list(range(8)))
    out = np.zeros((4, S, D), np.float32)
    for c in range(8):
        b, p = c // 2, c % 2
        o = np.asarray(res.results[c]["outT"]).T
        if p == 1:
            out[b, HALF:] = o
        else:
            out[b, :HALF] = o[::-1]
    return out
```

```python
import numpy as np
from contextlib import ExitStack
import concourse.bass as bass
import concourse.mybir as mybir
from concourse.bass_utils import run_bass_kernel_spmd

F32 = mybir.dt.float32
BF16 = mybir.dt.bfloat16
AF = mybir.ActivationFunctionType
ALU = mybir.AluOpType

D = 1024
S = 8192
LW = 1280
NH = 10
PW = 1024
DFF = 2816
NFF = 22
EPS = 1e-6
WINS = (2, 4, 8, 16)

PADL = 32
TOT = S + 64
HALF = S // 2
V1 = 482
NW1 = 17
V2 = 456
NW2 = 9
NB = 488
NB2 = 460
PIECE = 4096
NPIECE = 33
RSLOTS = 5
NSET = 4
WSEQ = [("b", 0), ("g", 0), ("g", 1), ("b", 1), ("g", 2), ("b", 2), ("g", 3), ("b", 3),
        ("g", 4), ("b", 4), ("g", 5), ("b", 5), ("g", 6), ("b", 6), ("g", 7), ("b", 7)] \
    + [("g", k) for k in range(8, 16)]


class B:
    def __init__(self, key, ap):
        self.key = key
        self.ap = ap

    def __getitem__(self, idx):
        return B(self.key, self.ap[idx])


class Prog:
    ENGS = ("pe", "act", "dve", "pool", "sp")

    def __init__(self):
        self.ops = {e: [] for e in self.ENGS}
        self.count = {e: 0 for e in self.ENGS}
        self.writer = {}
        self.readers = {}
        self.waited = {e: {} for e in self.ENGS}
        self.dma_count = {}
        self.sems = set(self.ENGS)
        self.stage = ""
        self.tags = {e: [] for e in self.ENGS}

    def add(self, eng, fn, reads=(), writes=(), signal=True, stream=None):
        reads = list(reads)
        writes = list(writes)
        if stream is not None:
            writes.append("stream:" + stream)
        deps = []
        for k in reads + writes:
            t = self.writer.get(k)
            if t is not None:
                deps.append(t)
        for k in writes:
            deps.extend(self.readers.get(k, ()))
        if stream is not None:
            s = "dma:" + eng + ":" + stream
            self.sems.add(s)
            c = self.dma_count.get(s, 0) + 1
            self.dma_count[s] = c
            tok = (s, 16 * c)
            inc = (s, 16)
        elif signal:
            self.count[eng] += 1
            tok = (eng, self.count[eng])
            inc = (eng, 1)
        else:
            tok = (eng, self.count[eng] + 1)
            inc = None
        need = {}
        for (s, v) in deps:
            if eng == "pe" and s == "pe":
                continue
            if self.waited[eng].get(s, 0) >= v:
                continue
            if need.get(s, 0) < v:
                need[s] = v
        for s, v in need.items():
            self.waited[eng][s] = v
        self.ops[eng].append((list(need.items()), fn, inc))
        self.tags[eng].append(self.stage)
        for k in reads:
            self.readers.setdefault(k, []).append(tok)
        for k in writes:
            self.writer[k] = tok
            self.readers[k] = []
        return tok


def build_nc(STOP=99):
    nc = bass.Bass("TRN2", target_bir_lowering=False)

    def din(name, shape):
        return nc.dram_tensor(name, list(shape), F32, kind="ExternalInput").ap()

    xT = din("xT", [D, TOT])
    wstream_f = din("wstream_f", [NPIECE, 128, PIECE])
    wua_f = din("wua_f", [128, 8 * LW])
    wg1_f = din("wg1_f", [128, 2 * NH * 128])
    wg2_f = din("wg2_f", [128, 2 * NH * 128])
    vec = din("vec", [128, 512])
    mtab = din("mtab", [128, 8 * 16])
    ident = din("ident", [128, 128])
    outT = nc.dram_tensor("outT", [D, HALF], F32, kind="ExternalOutput").ap()
    wstream = nc.dram_tensor("wstream", [NPIECE, 128, PIECE], BF16, kind="Internal").ap()
    h1scr = nc.dram_tensor("h1scr", [LW, TOT], F32, kind="Internal").ap()
    xascr = nc.dram_tensor("xascr", [LW, TOT], F32, kind="Internal").ap()

    VC = {}
    off = 0
    for name, n in (("g_mix", 8), ("g_ffn", 8), ("g_final", 8), ("bg", 16), ("cab", 10), ("cat", 50),
                    ("ba1", 10), ("bx1", 10), ("lam1", 10), ("ba2", 10), ("bx2", 10), ("lam2", 10),
                    ("bpool", 8), ("pscale", 8), ("ktap", 8 * 17), ("cft", 44 * 3), ("cfb", 44)):
        VC[name] = off
        off += n
    assert off <= 512

    P = Prog()
    with ExitStack() as es:
        def sbt(name, shape, dt):
            return es.enter_context(nc.sbuf_tensor(name, list(shape), dt))

        vec_t = sbt("vec_t", [128, 512], F32)
        der_t = sbt("der_t", [128, 128], F32)
        wg2_t = sbt("wg2_t", [128, 2 * NH * 128], BF16)
        cst_t = sbt("cst_t", [128, 8], F32)
        ones_t = sbt("ones_t", [128, 128], BF16)
        wg1_t = sbt("wg1_t", [128, 2 * NH * 128], BF16)
        mtab_t = sbt("mtab_t", [128, 8 * 16], F32)
        xw_t = sbt("xw_t", [128, 8 * NB], F32)
        hT_t = sbt("hT_t", [128, 8 * NB], BF16)
        sq_t = [sbt(f"sq{i}", [128, NB], BF16) for i in range(2)]
        rstd_t = sbt("rstd_t", [128, NB], F32)
        lset = [{n: sbt(f"l{n}{s}", [128, NB], F32) for n in ("u", "xa", "tr", "ti", "a2", "h1")}
                for s in range(NSET)]
        lx16 = [sbt(f"lx16_{s}", [128, NB], BF16) for s in range(NSET)]
        lu16 = [sbt(f"lu16_{s}", [128, NB], BF16) for s in range(NSET)]
        ident_t = sbt("ident_t", [128, 128], F32)
        dg5_t = sbt("dg5_t", [128, 50 * 128], BF16)
        dg3_t = sbt("dg3_t", [128, 12 * 128], BF16)
        hsum_t = sbt("hsum_t", [128, NH * NB2], BF16)
        carry_t = sbt("carry_t", [128, 64], F32)
        pooled_t = sbt("pooled_t", [128, 8 * NB2], BF16)
        vb_t = sbt("vb_t", [128, 8 * NB2], BF16)
        tg_t = sbt("tg_t", [128, 16 * NB2], BF16)
        act_t = sbt("act_t", [128, NFF * V2], BF16)
        wslot_t = [sbt(f"wslot{i}", [128, PIECE], BF16) for i in range(RSLOTS)]
        ps_t = [es.enter_context(nc.psum_tensor(f"ps{i}", [128, 512], F32)) for i in range(8)]

        def T(key, t):
            return B(key, t[:])

        vecb = T("vec", vec_t)
        der = T("der", der_t)
        cst = T("cst", cst_t)
        ones = T("ones", ones_t)
        wg1 = T("wg1", wg1_t)
        wg2 = T("wg2", wg2_t)
        mtabb = T("mtab", mtab_t)
        xw = [B(f"xw{k}", xw_t[:, k * NB:(k + 1) * NB]) for k in range(8)]
        hT = [B(f"hT{k}", hT_t[:, k * NB:(k + 1) * NB]) for k in range(8)]
        hTb = [B(f"hTb{k}", tg_t[:, k * 2 * NB2:k * 2 * NB2 + NB]) for k in range(8)]
        sq = [T(f"sq{i}", sq_t[i]) for i in range(2)]
        rstd = T("rstd", rstd_t)
        LS = [{n: T(f"l{n}{s}", lset[s][n]) for n in lset[s]} for s in range(NSET)]
        for s in range(NSET):
            LS[s]["x16"] = T(f"lx16_{s}", lx16[s])
            LS[s]["u16"] = T(f"lu16_{s}", lu16[s])
        actf = act_t[:].bitcast(F32)
        poolf = pooled_t[:].bitcast(F32)
        vbf = vb_t[:].bitcast(F32)
        NS1 = NSET + 2
        LS1 = list(LS)
        s4 = {n_: B(f"l{n_}4", actf[:, i_ * NB:(i_ + 1) * NB]) for i_, n_ in enumerate(("xa", "tr", "ti", "a2", "h1"))}
        s4["x16"] = B("lx16_4", act_t[:, 10 * NB:11 * NB])
        s4["u16"] = B("lu16_4", act_t[:, 11 * NB:12 * NB])
        s5 = {n_: B(f"l{n_}5", poolf[:, i_ * NB:(i_ + 1) * NB]) for i_, n_ in enumerate(("xa", "tr", "ti"))}
        s5.update({n_: B(f"l{n_}5", vbf[:, i_ * NB:(i_ + 1) * NB]) for i_, n_ in enumerate(("a2", "h1"))})
        s5["x16"] = B("lx16_5", act_t[:, 12 * NB:13 * NB])
        s5["u16"] = B("lu16_5", act_t[:, 13 * NB:14 * NB])
        LS1 += [s4, s5]
        assert 14 * NB <= NFF * V2 and 3 * NB <= 4 * NB2
        tgf = tg_t[:].bitcast(F32)
        hsf = hsum_t[:].bitcast(F32)
        xal = [B(f"xal{k}", tgf[:, k * NB:(k + 1) * NB]) for k in range(7)] + [B("xal7", hsf[:, 0:NB])]
        rstd2 = B("rstd2", vbf[:, 0:NB])
        assert 7 * NB <= 8 * NB2
        PF_REAL = [f"tg{k}" for k in range(16)] + [f"hsum{j_}" for j_ in range(NH)] + [f"vb{k}" for k in range(8)]
        PF_ALIAS = [f"xal{k}" for k in range(8)] + ["rstd2"]
        hsum = [B(f"hsum{j}", hsum_t[:, j * NB2:(j + 1) * NB2]) for j in range(NH)]
        carry = T("carry", carry_t)
        identb = T("ident", ident_t)
        dg5 = [B(f"dg5_{i}", dg5_t[:, i * 128:(i + 1) * 128]) for i in range(50)]
        dg3 = [B(f"dg3_{i}", dg3_t[:, i * 128:(i + 1) * 128]) for i in range(12)]
        dg_rr = [0, 0]
        pooled = [B(f"pooled{k}", pooled_t[:, k * NB2:(k + 1) * NB2]) for k in range(8)]
        vb = [B(f"vb{k}", vb_t[:, k * NB2:(k + 1) * NB2]) for k in range(8)]
        tg = [B(f"tg{k}", tg_t[:, k * NB2:(k + 1) * NB2]) for k in range(16)]
        actb = [B(f"act{j}", act_t[:, j * V2:(j + 1) * V2]) for j in range(NFF)]
        wslot = [T(f"wslot{i}", wslot_t[i]) for i in range(RSLOTS)]
        psb = [T(f"ps{i}", ps_t[i]) for i in range(8)]
        ps_rr = [0]

        def PS():
            b = psb[ps_rr[0] % 6]
            ps_rr[0] += 1
            return b

        def vcol(name, i=0):
            c = VC[name] + i
            return vecb[:, c:c + 1]

        DC = {"hc1": 0, "c1": 10, "hc2": 20, "c2": 30, "hba1": 40, "hbx1": 50, "hba2": 60, "hbx2": 70, "hbg": 80}

        def dcol(name, i):
            c = DC[name] + i
            return der[:, c:c + 1]

        C_EPS, C_Q, C_ONE, C_ZERO = 0, 1, 2, 3

        def ccol(i):
            return cst[:, i:i + 1]

        def keys(*xs):
            return [x.key for x in xs if isinstance(x, B)]

        def apv(x):
            return x.ap if isinstance(x, B) else x

        def mm(out, lhsT, rhs, start, stop, sig=False):
            P.add("pe", lambda e: e.matmul(out.ap, lhsT=lhsT.ap, rhs=rhs.ap, start=start, stop=stop),
                  reads=keys(lhsT, rhs), writes=keys(out), signal=(stop or sig))

        def act(out, in_, func, bias=None, scale=None):
            kw = {}
            if bias is not None:
                kw["bias"] = apv(bias)
            if scale is not None:
                kw["scale"] = apv(scale)
            P.add("act", lambda e: e.activation(out=out.ap, in_=in_.ap, func=func, **kw),
                  reads=keys(in_, bias, scale), writes=keys(out))

        def ts(out, in0, s1, s2, op0, op1=None, eng="dve"):
            if op1 is None:
                f = lambda e: e.tensor_scalar(out=out.ap, in0=in0.ap, scalar1=apv(s1), scalar2=None, op0=op0)
            else:
                f = lambda e: e.tensor_scalar(out=out.ap, in0=in0.ap, scalar1=apv(s1), scalar2=apv(s2),
                                              op0=op0, op1=op1)
            P.add(eng, f, reads=keys(in0, s1, s2), writes=keys(out))

        def stt(out, in0, sc, in1, op0, op1):
            P.add("dve", lambda e: e.scalar_tensor_tensor(out=out.ap, in0=in0.ap, scalar=apv(sc), in1=in1.ap,
                                                          op0=op0, op1=op1),
                  reads=keys(in0, sc, in1), writes=keys(out))

        def tt(out, in0, in1, op, eng="dve"):
            P.add(eng, lambda e: e.tensor_tensor(out=out.ap, in0=in0.ap, in1=in1.ap, op=op),
                  reads=keys(in0, in1), writes=keys(out))

        def cp(out, in_, eng="dve"):
            P.add(eng, lambda e: e.tensor_copy(out=out.ap, in_=in_.ap), reads=keys(in_), writes=keys(out))

        def recip(out, in_):
            P.add("dve", lambda e: e.reciprocal(out=out.ap, in_=in_.ap), reads=keys(in_), writes=keys(out))

        def mset(out, val, eng="dve"):
            P.add(eng, lambda e: e.memset(out.ap, val), writes=keys(out))

        def scan(out, d0, d1, init):
            P.add("dve", lambda e: e.tensor_tensor_scan(out=out.ap, data0=d0.ap, data1=d1.ap, initial=apv(init),
                                                        op0=ALU.mult, op1=ALU.add),
                  reads=keys(d0, d1, init), writes=keys(out))

        def dma(eng, out, in_, stream, extra_reads=(), extra_writes=()):
            P.add(eng, lambda e: e.dma_start(out=out.ap, in_=in_.ap),
                  reads=keys(in_) + list(extra_reads), writes=keys(out) + list(extra_writes), stream=stream)

        dma("sp", vecb, B("d:vec", vec[:, :]), "vec")
        dma("sp", mtabb, B("d:mtab", mtab[:, :]), "mtab")
        dma("sp", identb, B("d:ident", ident[:, :]), "ident")
        dma("pool", wg1, B("d:wg1", wg1_f[:, :]), "wg1")
        dma("pool", wg2, B("d:wg2", wg2_f[:, :]), "wg2")
        for i, (k0, k1) in enumerate(((0, 3), (3, 6), (6, 8))):
            dma("pool", wslot[i][:, 0:(k1 - k0) * LW], B("d:wua", wua_f[:, k0 * LW:k1 * LW]), f"wslot{i}")
        def prologue_cast(i):
            dma("pool", B(f"d:ws{i}", wstream[i]), B("d:wsf", wstream_f[i]), f"pro{i % 4}")
        for i_ in range(50):
            ts(dg5[i_], identb, vcol("cat", i_), None, ALU.mult)
        mset(cst[:, C_EPS:C_EPS + 1], EPS)
        mset(cst[:, C_Q:C_Q + 1], 0.25)
        mset(cst[:, C_ONE:C_ONE + 1], 1.0)
        mset(cst[:, C_ZERO:C_ZERO + 1], 0.0)
        mset(ones, 1.0)
        mset(carry, 0.0)
        for di, nm in ((1, "lam1"), (2, "lam2")):
            lam = vecb[:, VC[nm]:VC[nm] + 10]
            cdst = der[:, DC[f"c{di}"]:DC[f"c{di}"] + 10]
            hdst = der[:, DC[f"hc{di}"]:DC[f"hc{di}"] + 10]
            act(cdst, lam, AF.Exp, scale=-1.0)
            act(cdst, cdst, AF.Ln, bias=ccol(C_ONE))
            ts(hdst, cdst, -4.0, None, ALU.mult)
            ts(cdst, cdst, -8.0, None, ALU.mult)
        for nm_, cnt in (("ba1", 10), ("bx1", 10), ("ba2", 10), ("bx2", 10), ("bg", 16)):
            ts(der[:, DC["h" + nm_]:DC["h" + nm_] + cnt], vecb[:, VC[nm_]:VC[nm_] + cnt], 0.5, None, ALU.mult)

        ssb = psb[7]
        ssb2 = psb[6]
        sq_rr = [0]

        def ss_add(src_k, n, first, last, bank=None):
            P.stage = "norm"
            bank = ssb if bank is None else bank
            s_ = sq[sq_rr[0] % 2]
            sq_rr[0] += 1
            act(s_[:, 0:n], src_k, AF.Square)
            mm(bank[:, 0:n], ones, s_[:, 0:n], first, last, sig=True)

        def ss_finish(src, n, gname, dst, bank=None, rs=None):
            P.stage = "norm"
            bank = ssb if bank is None else bank
            rs = rstd if rs is None else rs
            act(rs[:, 0:n], bank[:, 0:n], AF.Sqrt, bias=ccol(C_EPS), scale=1.0 / D)
            recip(rs[:, 0:n], rs[:, 0:n])
            for k in range(8):
                stt(dst[k], src[k], vcol(gname, k), rs[:, 0:n], ALU.mult, ALU.mult)

        def rmsnorm(src, n, gname, dst, dst_off=0):
            for k in range(8):
                ss_add(src[k], n, k == 0, k == 7)
            ss_finish(src, n, gname, dst)

        def diag5(j, o):
            return dg5[j * 5 + o]

        def diag3(ch, o):
            dg = dg3[dg_rr[1] % 12]
            dg_rr[1] += 1
            ts(dg, identb, vcol("cft", ch * 3 + o), None, ALU.mult)
            return dg

        def lru_S0(ps, st, nu, on_act=False):
            P.stage = "S0"
            if on_act:
                act(st["u16"][:, 0:nu], ps[:, 0:nu], AF.Copy)
            else:
                cp(st["u16"][:, 0:nu], ps[:, 0:nu])

        def lru_S1(st, j, nout, u_off):
            P.stage = "S1conv5"
            cps = PS()
            for o in range(5):
                dg = diag5(j, o)
                mm(cps[:, 0:nout], dg, st["u16"][:, u_off - 2 + o: u_off - 2 + o + nout], o == 0, o == 4)
            act(st["xa"][:, 0:nout], cps[:, 0:nout], AF.Identity, bias=vcol("cab", j))
            cp(st["x16"][:, 0:nout], st["xa"][:, 0:nout])

        def lru_B1(st, j, nout, wa, wx, di, on_act=False):
            P.stage = "S2gates"
            pr = PS()
            pi = PS()
            mm(pr[:, 0:nout], wa, st["x16"][:, 0:nout], True, True)
            mm(pi[:, 0:nout], wx, st["x16"][:, 0:nout], True, True)
            act(st["tr"][:, 0:nout], pr[:, 0:nout], AF.Tanh, bias=dcol(f"hba{di}", j), scale=0.5)
            act(st["ti"][:, 0:nout], pi[:, 0:nout], AF.Tanh, bias=dcol(f"hbx{di}", j), scale=0.5)
            act(st["tr"][:, 0:nout], st["tr"][:, 0:nout], AF.Exp, bias=dcol(f"hc{di}", j), scale=dcol(f"hc{di}", j))
            if on_act:
                act(st["a2"][:, 0:nout], st["tr"][:, 0:nout], AF.Square)
            else:
                tt(st["a2"][:, 0:nout], st["tr"][:, 0:nout], st["tr"][:, 0:nout], ALU.mult)
            stt(st["ti"][:, 0:nout], st["ti"][:, 0:nout], 1.0, st["xa"][:, 0:nout], ALU.add, ALU.mult)

        def lru_B2(st, nout):
            P.stage = "S3scan"
            act(st["a2"][:, 0:nout], st["a2"][:, 0:nout], AF.Sqrt, bias=ccol(C_Q), scale=-0.25)
            tt(st["a2"][:, 0:nout], st["a2"][:, 0:nout], st["ti"][:, 0:nout], ALU.mult)

        def wua(k, j):
            i, kk = (0, k) if k < 3 else ((1, k - 3) if k < 6 else (2, k - 6))
            return wslot[i][:, kk * LW + j * 128: kk * LW + (j + 1) * 128]

        h1_store_wins = {}
        nw1 = NW1 if STOP >= 2 else (1 if STOP == 1 else 0)

        def s1_geom(w):
            v0 = V1 * w
            v1 = min(v0 + V1, S)
            return v0, v1, v1 - v0, v1 - v0 + 4, v0 - 2 + PADL

        def s1_load(w):
            v0, v1, nv, n, c0 = s1_geom(w)
            for k in range(8):
                dma("sp", xw[k][:, 0:n], B("d:x", xT[k * 128:(k + 1) * 128, c0:c0 + n]), f"xw{k}")

        def s1_norm(w):
            v0, v1, nv, n, c0 = s1_geom(w)
            hs = hT if w % 2 == 0 else hTb
            rmsnorm([xw[k][:, 0:n] for k in range(8)], n, "g_mix", [hs[k][:, 0:n] for k in range(8)])

        if nw1:
            s1_load(0)
            s1_norm(0)
        for w in range(nw1):
            v0, v1, nv, n, c0 = s1_geom(w)
            hTw = hT if w % 2 == 0 else hTb
            if w + 1 < nw1:
                s1_load(w + 1)
            for i_ in range(3 * w, min(3 * w + 3, NPIECE)):
                prologue_cast(i_)
            store = v1 > HALF - 32

            def S0(j):
                st = LS1[j % NS1]
                ps = PS()
                for k in range(8):
                    mm(ps[:, 0:n], wua(k, j), hTw[k][:, 0:n], k == 0, k == 7)
                lru_S0(ps, st, n)

            def S1(j):
                st = LS1[j % NS1]
                lru_S1(st, j, nv, 2)
                if store:
                    dma("pool", B(f"d:xa:{j}:{w}", xascr[j * 128:(j + 1) * 128, v0 + PADL:v1 + PADL]),
                        st["xa"][:, 0:nv], f"lxa{j % NS1}")

            def S2(j):
                st = LS1[j % NS1]
                lru_B1(st, j, nv, wg1[:, j * 128:(j + 1) * 128], wg1[:, (NH + j) * 128:(NH + j + 1) * 128], 1)

            def S3(j):
                st = LS1[j % NS1]
                lru_B2(st, nv)
                scan(st["h1"][:, 0:nv], st["tr"][:, 0:nv], st["a2"][:, 0:nv], carry[:, j:j + 1])
                cp(carry[:, j:j + 1], st["h1"][:, nv - 1:nv])
                if store:
                    dma("pool", B(f"d:h1:{j}:{w}", h1scr[j * 128:(j + 1) * 128, v0 + PADL:v1 + PADL]),
                        st["h1"][:, 0:nv], f"lh1{j % NS1}")

            steps = []
            npair = NH // 2
            for q in range(npair + 2):
                if q < npair:
                    steps += [(S0, 2 * q), (S0, 2 * q + 1)]
                if 0 <= q - 1 < npair:
                    steps += [(S1, 2 * q - 2), (S1, 2 * q - 1)]
                if 0 <= q - 2 < npair:
                    steps += [(S2, 2 * q - 4), (S2, 2 * q - 3), (S3, 2 * q - 4), (S3, 2 * q - 3)]
            if STOP == 1:
                import os
                steps = steps[:int(os.environ.get("SUB", "99"))]
            for si_, (fn_, j_) in enumerate(steps):
                if si_ == 26 and w + 1 < nw1:
                    s1_norm(w + 1)
                fn_(j_)
            if store:
                h1_store_wins[w] = (v0, v1)
        for i_ in range(3 * nw1, NPIECE):
            prologue_cast(i_)
        P.add("dve", lambda e: e.memset(cst_t[:, 7:8], 0.0),
              writes=[f"hTb{k}" for k in range(8)] + [f"tg{k}" for k in range(16)] + ["cst7"]
              + [b_.key for b_ in list(s4.values()) + list(s5.values())]
              + [f"act{j_}" for j_ in range(NFF)] + [f"pooled{k}" for k in range(8)] + [f"vb{k}" for k in range(8)])

        class WStream:
            def __init__(self, total):
                self.total = total
                self.nxt = 0
                self.loaded = 0
                for _ in range(RSLOTS - 1):
                    self._load()

            def _load(self):
                p = self.loaded
                if p >= self.total:
                    return
                pi = p % NPIECE
                dma("sp", wslot[p % RSLOTS], B(f"d:ws{pi}", wstream[pi]), f"wslot{p % RSLOTS}")
                self.loaded += 1

            def get(self):
                p = self.nxt
                while self.loaded < min(p + RSLOTS - 1, self.total) or self.loaded <= p:
                    self._load()
                self.nxt += 1
                return wslot[p % RSLOTS]

        WS = WStream(NW2 * NPIECE) if STOP >= 3 else None

        mset(carry[:, 16:16 + NH], 0.0)
        out_keys = []
        for w in range(NW2 if STOP >= 4 else (1 if STOP == 3 else 0)):
            v1 = S - V2 * w
            v0 = max(v1 - V2, HALF)
            nf = v1 - v0
            nm = nf + 2
            n = nf + 18
            c0 = v0 - 9 + PADL
            top = (w == 0)
            ns = nm - 1 if top else nm - 2
            nh = nm - 1 if top else nm
            for k in range(8):
                dma("sp", xw[k][:, 0:n], B("d:x", xT[k * 128:(k + 1) * 128, c0:c0 + n]), f"xw{k}")
            if w == 0:
                rmsnorm([xw[k][:, 0:n] for k in range(8)], n, "g_mix", [hT[k][:, 0:n] for k in range(8)])
            nw2 = NW2 if STOP >= 4 else (1 if STOP == 3 else 0)
            has_next = (w + 1 < nw2)
            if has_next:
                v1n = S - V2 * (w + 1)
                v0n = max(v1n - V2, HALF)
                nn = v1n - v0n + 18
                c0n = v0n - 9 + PADL

            wpiece = [None]

            def win_chunk(oc, n=n, wpiece=wpiece):
                P.stage = "win"
                if oc % 4 == 0:
                    wpiece[0] = WS.get()
                ps = PS()
                q = oc % 4
                for k in range(8):
                    mm(ps[:, 0:n], wpiece[0][:, k * 512 + q * 128: k * 512 + (q + 1) * 128], hT[k][:, 0:n],
                       k == 0, k == 7)
                return ps

            def SL(j):
                P.stage = "S0"
                st = LS[j % NSET]
                t0 = v0 - 1
                t1 = t0 + nh
                rk = [f"d:xa:{j}:{ww}" for ww, (a0, a1) in h1_store_wins.items() if a0 < t1 and a1 > t0]
                dma("sp", st["xa"][:, 0:nh], B("d:xascr", xascr[j * 128:(j + 1) * 128, t0 + PADL:t1 + PADL]),
                    f"lxa{j % NSET}", extra_reads=rk)
                if top:
                    mset(st["xa"][:, nh:nm], 0.0)
                cp(st["x16"][:, 0:nm], st["xa"][:, 0:nm])

            def S2(j):
                st = LS[j % NSET]
                lru_B1(st, j, nm, wg2[:, j * 128:(j + 1) * 128], wg2[:, (NH + j) * 128:(NH + j + 1) * 128], 2,
                       on_act=True)

            def S3(j):
                st = LS[j % NSET]
                lru_B2(st, nm)
                t0 = v0 - 1
                t1 = t0 + nh
                rk = [f"d:h1:{j}:{ww}" for ww, (a0, a1) in h1_store_wins.items() if a0 < t1 and a1 > t0]
                dma("sp", st["h1"][:, 0:nh], B("d:h1scr", h1scr[j * 128:(j + 1) * 128, t0 + PADL:t1 + PADL]),
                    f"lh1{j % NSET}", extra_reads=rk)
                h2 = st["u"]
                scan(h2[:, 0:ns][:, ::-1], st["tr"][:, 0:ns][:, ::-1], st["a2"][:, 0:ns][:, ::-1],
                     carry[:, 16 + j:17 + j])
                if not top:
                    cp(h2[:, ns:nm], carry[:, 32 + 2 * j:34 + 2 * j])
                cp(carry[:, 16 + j:17 + j], h2[:, 0:1])
                cp(carry[:, 32 + 2 * j:34 + 2 * j], h2[:, 0:2])
                tt(hsum[j][:, 0:nh], h2[:, 0:nh], st["h1"][:, 0:nh], ALU.add)
                if top:
                    mset(hsum[j][:, nh:nm], 0.0)

            def pool_chunk(pos, c):
                ps = win_chunk(pos)
                P.stage = "pool"
                u_ = LS[c % NSET]["u"]
                Sb = LS[c % NSET]["ti"]
                act(u_[:, 0:n], ps[:, 0:n], AF.Copy)
                wv = WINS[c // 2]
                bufs = [LS[c % NSET]["tr"], LS[c % NSET]["a2"]]
                cur, width, span, bi = u_, n, 1, 0
                while span < wv:
                    nxt = bufs[bi]
                    bi ^= 1
                    width -= span
                    tt(nxt[:, 0:width], cur[:, 0:width], cur[:, span:span + width], ALU.add)
                    cur = nxt
                    span *= 2
                b0 = 8 - wv // 2
                ts(Sb[:, 0:nm], cur[:, b0:b0 + nm], vcol("ktap", 2 * c), None, ALU.mult)
                stt(Sb[:, 0:nm], cur[:, b0 + 1:b0 + 1 + nm], vcol("ktap", 2 * c + 1), Sb[:, 0:nm], ALU.mult, ALU.add)
                if top:
                    tt(Sb[:, nm - 16:nm], Sb[:, nm - 16:nm], mtabb[:, c * 16:(c + 1) * 16], ALU.mult)
                tt(pooled[c][:, 0:nm], Sb[:, 0:nm], u_[:, 8:8 + nm], ALU.subtract)

            def gate_chunk(pos, k):
                ps = win_chunk(pos)
                P.stage = "tg"
                act(tg[k][:, 0:nm], ps[:, 8:8 + nm], AF.Tanh, bias=dcol("hbg", k), scale=0.5)

            for q in range(NH // 2 + 1):
                if q < NH // 2:
                    SL(2 * q)
                    SL(2 * q + 1)
                for pos in range(4 * q, 4 * q + 4):
                    kind, idx = WSEQ[pos]
                    (pool_chunk if kind == "b" else gate_chunk)(pos, idx)
                if q >= 1:
                    S2(2 * q - 2)
                    S2(2 * q - 1)
                    S3(2 * q - 2)
                    S3(2 * q - 1)

            P.stage = "ya"
            for i in range(3):
                wp = WS.get()
                for dd in range(3):
                    d = 3 * i + dd
                    if d >= 8:
                        continue
                    ps = PS()
                    for k in range(NH):
                        lo = k * 384 + dd * 128
                        mm(ps[:, 0:nm], wp[:, lo:lo + 128], hsum[k][:, 0:nm], k == 0, k == NH - 1)
                    stt(actb[d][:, 0:nm - 2], tg[d][:, 0:nm - 2], 1.0, ps[:, 0:nm - 2], ALU.add, ALU.mult)
                    stt(actb[8 + d][:, 0:2], tg[d][:, nm - 2:nm], 1.0, ps[:, nm - 2:nm], ALU.add, ALU.mult)
            P.stage = "wpool"
            wp = WS.get()
            for g in range(4):
                for oc in range(2):
                    ps = PS()
                    for kc in range(2):
                        lo = (g * 2 + kc) * 256 + oc * 128
                        mm(ps[:, 0:nm], wp[:, lo:lo + 128], pooled[2 * g + kc][:, 0:nm], kc == 0, kc == 1)
                    d = 2 * g + oc
                    ts(vb[d][:, 0:nm], ps[:, 0:nm], vcol("bpool", d), vcol("pscale", d), ALU.add, ALU.mult)

            P.stage = "yb"
            for i in range(2):
                wp = WS.get()
                for dd in range(4):
                    d = 4 * i + dd
                    ps = PS()
                    for k in range(8):
                        lo = k * 512 + dd * 128
                        mm(ps[:, 0:nm], wp[:, lo:lo + 128], vb[k][:, 0:nm], k == 0, k == 7)
                    m_ = LS[d % NSET]["tr"]
                    stt(m_[:, 0:nm], tg[8 + d][:, 0:nm], 1.0, ps[:, 0:nm], ALU.add, ALU.mult)
                    tt(hsum[d][:, 0:nm - 2], m_[:, 0:nm - 2], actb[d][:, 0:nm - 2], ALU.add)
                    tt(hsum[d][:, nm - 2:nm], m_[:, nm - 2:nm], actb[8 + d][:, 0:2], ALU.add)
            P.stage = "wout"
            for i in range(2):
                wp = WS.get()
                for dd in range(4):
                    d = 4 * i + dd
                    ps = PS()
                    for k in range(8):
                        lo = k * 512 + dd * 128
                        mm(ps[:, 0:nm], wp[:, lo:lo + 128], hsum[k][:, 0:nm], k == 0, k == 7)
                    stt(xw[d][:, 8:8 + nm], ps[:, 0:nm], 0.5, xw[d][:, 8:8 + nm], ALU.mult, ALU.add)
                    if d >= 2:
                        ss_add(xw[d - 2][:, 8:8 + nm], nm, d == 2, False)
                    P.stage = "wout"
            ss_add(xw[6][:, 8:8 + nm], nm, False, False)
            ss_add(xw[7][:, 8:8 + nm], nm, False, True)

            ss_finish([xw[k][:, 8:8 + nm] for k in range(8)], nm, "g_ffn", [hT[k][:, 0:nm] for k in range(8)])
            if top:
                for k in range(8):
                    mset(hT[k][:, nm - 1:nm], 0.0)
            if has_next:
                for k in range(8):
                    dma("sp", xal[k][:, 0:nn], B("d:x", xT[k * 128:(k + 1) * 128, c0n:c0n + nn]), f"xal{k}",
                        extra_writes=(PF_REAL + ["rstd2"]) if k == 0 else [])
            ups = {}
            cres = {}
            wpf = [None]

            def FU(h):
                if h % 4 == 0:
                    wpf[0] = WS.get()
                P.stage = "up"
                jj, half = h // 2, h % 2
                q = half * 2 + (jj % 2)
                ps = PS()
                for k in range(8):
                    mm(ps[:, 0:nm], wpf[0][:, k * 512 + q * 128:k * 512 + (q + 1) * 128], hT[k][:, 0:nm],
                       k == 0, k == 7)
                u16 = LS[h % NSET]["u16"]
                act(u16[:, 0:nm], ps[:, 0:nm], AF.Copy)
                ups[h] = u16

            dgs = {}

            def FD(h):
                P.stage = "conv3"
                jj, half = h // 2, h % 2
                ch = half * NFF + jj
                dgs[h] = [diag3(ch, o) for o in range(3)]

            def FC(h):
                P.stage = "conv3"
                c_ = PS()
                for o in range(3):
                    mm(c_[:, 0:nf], dgs[h][o], ups[h][:, o:o + nf], o == 0, o == 2)
                cres[h] = c_

            def FE(jj):
                P.stage = "ffn_e"
                gl = LS[jj % NSET]["xa"]
                act(gl[:, 0:nf], cres[2 * jj][:, 0:nf], AF.Gelu_apprx_tanh, bias=vcol("cfb", jj))
                stt(actb[jj][:, 0:nf], cres[2 * jj + 1][:, 0:nf], vcol("cfb", NFF + jj), gl[:, 0:nf],
                    ALU.add, ALU.mult)

            FD(0)
            FD(1)
            FU(0)
            FU(1)
            for h in range(2 * NFF):
                if h + 2 < 2 * NFF:
                    FD(h + 2)
                    FU(h + 2)
                FC(h)
                if h % 2 == 1:
                    FE(h // 2)
            if has_next:
                for k in range(8):
                    ss_add(xal[k][:, 0:nn], nn, k == 0, k == 7, bank=ssb2)
                ss_finish([xal[k][:, 0:nn] for k in range(8)], nn, "g_mix", [hT[k][:, 0:nn] for k in range(8)],
                          bank=ssb2, rs=rstd2)
                P.add("dve", lambda e: e.memset(cst_t[:, 6:7], 0.0), writes=PF_REAL + PF_ALIAS + ["cst6"])
            P.stage = "down"
            for d in range(8):
                wp = WS.get()
                ps = PS()
                for kc in range(NFF):
                    mm(ps[:, 0:nf], wp[:, kc * 128:(kc + 1) * 128], actb[kc][:, 0:nf], kc == 0, kc == NFF - 1)
                tt(xw[d][:, 9:9 + nf], ps[:, 0:nf], xw[d][:, 9:9 + nf], ALU.add)
                if d >= 1:
                    ss_add(xw[d - 1][:, 9:9 + nf], nf, d == 1, False)
                P.stage = "down"
            ss_add(xw[7][:, 9:9 + nf], nf, False, True)
            ss_finish([xw[k][:, 9:9 + nf] for k in range(8)], nf, "g_final", [xw[k][:, 9:9 + nf] for k in range(8)])
            for k in range(8):
                ok = f"d:out:{w}:{k}"
                out_keys.append(ok)
                dma("pool", B(ok, outT[k * 128:(k + 1) * 128, v0 - HALF:v1 - HALF]), xw[k][:, 9:9 + nf], f"xw{k}")

        P.add("pool", lambda e: None, reads=out_keys + [k for k in P.writer if k.startswith("stream:")],
              signal=False)

        sems = {s: es.enter_context(nc.semaphore(s.replace(":", "_"))) for s in sorted(P.sems)}
        block = es.enter_context(nc.Block())

        def runner(engops):
            def f(e):
                for waits, fn, inc in engops:
                    for s, v in waits:
                        e.wait_ge(sems[s], v)
                    ins = fn(e)
                    if inc is not None and ins is not None:
                        ins.then_inc(sems[inc[0]], inc[1])
            return f

        block.sync(runner(P.ops["sp"]))
        block.scalar(runner(P.ops["act"]))
        block.vector(runner(P.ops["dve"]))
        block.gpsimd(runner(P.ops["pool"]))
        block.tensor(runner(P.ops["pe"]))
        print("ops:", {k: len(v) for k, v in P.ops.items()}, "sems:", len(sems), flush=True)
        import os
        if os.environ.get("DUMP_TAGS"):
            import json
            json.dump(P.tags, open(os.environ["DUMP_TAGS"], "w"))
    return nc


def _cols(v, nch):
    return np.ascontiguousarray(np.asarray(v, np.float32).reshape(nch, 128).T)


def _wstream(w_in, w_pool, p_a, p_b, w_out, w_up, w_down):
    ws = np.zeros((NPIECE, 128, PIECE), np.float32)
    pi = 0
    wi = w_in[:, LW:].reshape(8, 128, -1)
    for i in range(6):
        blk = np.zeros((128, 8, 4, 128), np.float32)
        for q_, (kind, idx) in enumerate(WSEQ[4 * i:4 * i + 4]):
            c0 = idx * 128 if kind == "b" else PW + idx * 128
            blk[:, :, q_, :] = wi[:, :, c0:c0 + 128].transpose(1, 0, 2)
        ws[pi] = blk.reshape(128, -1)
        pi += 1
    pa = p_a.reshape(NH, 128, 8, 128)
    for i in range(3):
        blk = np.zeros((128, NH, 3, 128), np.float32)
        dn = min(3, 8 - 3 * i)
        blk[:, :, :dn] = pa[:, :, 3 * i:3 * i + dn].transpose(1, 0, 2, 3)
        ws[pi, :, :NH * 384] = blk.reshape(128, -1)
        pi += 1
    blk = w_pool.reshape(4, 2, 128, 256).transpose(2, 0, 1, 3)
    ws[pi, :, :2048] = blk.reshape(128, -1)
    pi += 1
    for m in (p_b, w_out):
        mm_ = m.reshape(8, 128, 2, 512)
        for i in range(2):
            ws[pi] = mm_[:, :, i].transpose(1, 0, 2).reshape(128, -1)
            pi += 1
    wu = w_up.reshape(8, 128, 2, NFF, 128)
    for i in range(11):
        blk = wu[:, :, :, 2 * i:2 * i + 2]
        ws[pi] = blk.transpose(1, 0, 2, 3, 4).reshape(128, -1)
        pi += 1
    wd = w_down.reshape(NFF, 128, 8, 128)
    for d in range(8):
        ws[pi, :, :NFF * 128] = wd[:, :, d].transpose(1, 0, 2).reshape(128, -1)
        pi += 1
    assert pi == NPIECE
    return ws


def kernel(x, g_mix, w_in, b_gate, conv_a_w, conv_a_b, wa_f, ba_f, wx_f, bx_f, lam_f,
           wa_b, ba_b, wx_b, bx_b, lam_b, w_pool, b_pool, pool_scale, p_a, p_b, w_out,
           g_ffn, w_up, conv_f_w, conv_f_b, w_down, g_final):
    f = lambda a: np.asarray(a, np.float32)
    x = f(x)
    w_in0 = f(w_in)[0]
    ws = _wstream(w_in0, f(w_pool)[0], f(p_a)[0], f(p_b)[0], f(w_out)[0], f(w_up)[0], f(w_down)[0])
    wua = np.ascontiguousarray(w_in0[:, :LW].reshape(8, 128, LW).transpose(1, 0, 2).reshape(128, -1))

    def gatepack(wa, wx):
        a = f(wa)[0].transpose(1, 0, 2)
        b = f(wx)[0].transpose(1, 0, 2)
        return np.ascontiguousarray(np.concatenate([a, b], axis=1).reshape(128, -1))

    in_maps = []
    for c in range(8):
        b, p = c // 2, c % 2
        seq = x[b] if p == 1 else x[b][::-1]
        xT = np.zeros((D, TOT), np.float32)
        xT[:, PADL:PADL + S] = seq.T
        if p == 1:
            d1 = (wa_f, ba_f, wx_f, bx_f, lam_f)
            d2 = (wa_b, ba_b, wx_b, bx_b, lam_b)
        else:
            d1 = (wa_b, ba_b, wx_b, bx_b, lam_b)
            d2 = (wa_f, ba_f, wx_f, bx_f, lam_f)
        caw = f(conv_a_w)[0]
        taps5 = np.zeros((5, LW), np.float32)
        if p == 1:
            taps5[0:4] = caw
        else:
            taps5[1:5] = caw[::-1]
        cfw = f(conv_f_w)[0]
        taps3 = cfw if p == 1 else cfw[::-1]
        vec = np.zeros((128, 512), np.float32)
        off = [0]

        def put(arr):
            a = np.asarray(arr, np.float32)
            vec[:, off[0]:off[0] + a.shape[1]] = a
            off[0] += a.shape[1]

        put(_cols(f(g_mix)[0], 8))
        put(_cols(f(g_ffn)[0], 8))
        put(_cols(f(g_final), 8))
        put(_cols(f(b_gate)[0], 16))
        put(_cols(f(conv_a_b)[0], 10))
        put(taps5.T.reshape(NH, 128, 5).transpose(1, 0, 2).reshape(128, 50))
        put(_cols(f(d1[1])[0], 10))
        put(_cols(f(d1[3])[0], 10))
        put(_cols(f(d1[4])[0], 10))
        put(_cols(f(d2[1])[0], 10))
        put(_cols(f(d2[3])[0], 10))
        put(_cols(f(d2[4])[0], 10))
        put(_cols(f(b_pool)[0], 8))
        put(_cols(f(pool_scale)[0], 8))
        kt = np.zeros((128, 8 * 17), np.float32)
        for cidx in range(8):
            wv = WINS[cidx // 2]
            kt[:, 2 * cidx + (0 if p == 1 else 1)] = 1.0 / wv
        put(kt.reshape(128, -1))
        put(taps3.T.reshape(2 * NFF, 128, 3).transpose(1, 0, 2).reshape(128, -1))
        put(_cols(f(conv_f_b)[0], 2 * NFF))
        mt = np.ones((128, 8, 16), np.float32)
        for cidx in range(8):
            wv = WINS[cidx // 2]
            for m in range(16):
                tau = S - 15 + m
                if tau >= S:
                    continue
                t = tau if p == 1 else S - 1 - tau
                lo = min(max(t - wv // 2, 0), S - 1)
                hi = min(max(t + wv // 2 - 1, 0), S - 1)
                mt[:, cidx, m] = wv / float(hi - lo + 1)
        in_maps.append({
            "xT": xT, "wstream_f": ws, "wua_f": wua,
            "wg1_f": gatepack(d1[0], d1[2]), "wg2_f": gatepack(d2[0], d2[2]),
            "vec": vec, "mtab": np.ascontiguousarray(mt.reshape(128, -1)),
            "ident": np.eye(128, dtype=np.float32),
        })
    nc = build_nc()
    res = run_bass_kernel_spmd(nc, in_maps, core_ids=list(range(8)))
    out = np.zeros((4, S, D), np.float32)
    for c in range(8):
        b, p = c // 2, c % 2
        o = np.asarray(res.results[c]["outT"]).T
        if p == 1:
            out[b, HALF:] = o
        else:
            out[b, :HALF] = o[::-1]
    return out
```
